# Optimizing a Trainium2 kernel written in Bass

```python
import math
import jax, jax.numpy as jnp
from jax import lax
import numpy as np

D_MODEL = 1024
BATCH = 4
SEQ = 4096
DEPTH = 2
DEC_BATCH = 128
DEC_SEQ = 4
PAST_LEN = 8192
PAGE_SIZE = 128

N_MIXERS = 2
N_ATTN_LAYERS = (DEPTH + N_MIXERS - 1) // N_MIXERS
N_RWKV_LAYERS = DEPTH // N_MIXERS
N_META = 16
RMS_EPS = 1e-6

A_HEADS = 16
A_KV_HEADS = 4
A_GROUP = A_HEADS // A_KV_HEADS
A_HEAD_DIM = 64
A_WIDTH = A_HEADS * A_HEAD_DIM
A_KV_WIDTH = A_KV_HEADS * A_HEAD_DIM
WINDOW = 128
BLOCK = 128
N_BUCKETS = 32
MAX_DISTANCE = 128

R_HEAD_DIM = 64
R_HEADS = D_MODEL // R_HEAD_DIM
R_WIDTH = R_HEADS * R_HEAD_DIM
DECAY_LORA = 64
ICLR_LORA = 64
GN_EPS = 64e-5
N_SHIFT_MIX = 6

kernel_name = "hybrid_swa_sink_rwkv7_step"


def rmsnorm(x, g):
    xf = x.astype(jnp.float32)
    y = xf * lax.rsqrt(jnp.mean(xf * xf, axis=-1, keepdims=True) + RMS_EPS)
    return (y * g.astype(jnp.float32)).astype(x.dtype)


def t5_bucket(rel):
    n = jnp.maximum(rel, 0)
    max_exact = N_BUCKETS // 2
    nf = jnp.maximum(n, max_exact).astype(jnp.float32)
    large = max_exact + (jnp.log(nf / max_exact) / math.log(MAX_DISTANCE / max_exact)
                         * (N_BUCKETS - max_exact)).astype(jnp.int32)
    large = jnp.minimum(large, N_BUCKETS - 1)
    return jnp.where(n < max_exact, n, large)


def rel_bias(table, rel):
    b = table[t5_bucket(rel)].astype(jnp.float32)
    return jnp.moveaxis(b, -1, 0).reshape(A_KV_HEADS, A_GROUP, *rel.shape)


def sink_attend(q, k, v, bias, valid, sinks):
    s = jnp.einsum('...qkgd,...skd->...kgqs', q, k).astype(jnp.float32) * (A_HEAD_DIM ** -0.5) + bias
    s = jnp.where(valid[..., None, None, :, :], s, -jnp.inf)
    sink = sinks.astype(jnp.float32).reshape(A_KV_HEADS, A_GROUP, 1, 1)
    m = jnp.maximum(jnp.max(s, axis=-1, keepdims=True), sink)
    p = jnp.exp(s - m)
    denom = jnp.sum(p, axis=-1, keepdims=True) + jnp.exp(sink - m)
    o = jnp.einsum('...kgqs,...skd->...qkgd', (p / denom).astype(v.dtype), v)
    return o.reshape(*o.shape[:-3], A_WIDTH)


def attn_project(xn, w_in):
    B, T = xn.shape[:2]
    proj = jnp.einsum('btd,de->bte', xn, w_in)
    q, k, v, g = jnp.split(proj, [A_WIDTH, A_WIDTH + A_KV_WIDTH, A_WIDTH + 2 * A_KV_WIDTH], axis=-1)
    return (q.reshape(B, T, A_KV_HEADS, A_GROUP, A_HEAD_DIM),
            k.reshape(B, T, A_KV_HEADS, A_HEAD_DIM),
            v.reshape(B, T, A_KV_HEADS, A_HEAD_DIM), g)


def attn_prompt(xn, w_in, sinks, w_out, table):
    q, k, v, g = attn_project(xn, w_in)
    B, L = xn.shape[:2]
    pad = BLOCK - N_META
    nb = (L + pad) // BLOCK

    def blocks(t):
        t = jnp.pad(t, [(0, 0), (pad, 0)] + [(0, 0)] * (t.ndim - 2))
        return t.reshape(B, nb, BLOCK, *t.shape[2:])

    def with_prev(t):
        prev = jnp.pad(t, [(0, 0), (1, 0)] + [(0, 0)] * (t.ndim - 2))[:, :-1]
        return jnp.concatenate([prev, t], axis=2)

    qb = blocks(q)
    kc = with_prev(blocks(k))
    vc = with_prev(blocks(v))
    qi = jnp.arange(BLOCK)[:, None]
    kj = jnp.arange(2 * BLOCK)[None, :]
    rel = BLOCK + qi - kj
    kpos = (jnp.arange(nb)[:, None, None] - 1) * BLOCK + kj[None] - pad
    valid = (rel >= 0) & (rel < WINDOW) & (kpos >= 0)
    o = sink_attend(qb, kc, vc, rel_bias(table, rel), valid, sinks)
    o = o.reshape(B, nb * BLOCK, A_WIDTH)[:, pad:]
    y = jnp.einsum('bte,ed->btd', o * jax.nn.silu(g), w_out)
    return y, k[:, -WINDOW:], v[:, -WINDOW:]


def attn_sample(xn, cache_k, cache_v, w_in, sinks, w_out, table):
    q, k, v, g = attn_project(xn, w_in)
    T = xn.shape[1]
    keep = cache_k.shape[1]
    kc = jnp.concatenate([cache_k.astype(k.dtype), k], axis=1)
    vc = jnp.concatenate([cache_v.astype(v.dtype), v], axis=1)
    rel = keep + jnp.arange(T)[:, None] - jnp.arange(keep + T)[None, :]
    valid = (rel >= 0) & (rel < WINDOW)
    o = sink_attend(q, kc, vc, rel_bias(table, rel), valid, sinks)
    y = jnp.einsum('bte,ed->btd', o * jax.nn.silu(g), w_out)
    return y, kc[:, -keep:], vc[:, -keep:]


def wkv_scan(S0, r, w, k, v, a, b):
    xs = tuple(jnp.moveaxis(t.astype(jnp.float32), 1, 0) for t in (r, w, k, v, a, b))

    def step(S, inp):
        r_t, w_t, k_t, v_t, a_t, b_t = inp
        sa = jnp.einsum('bhvk,bhk->bhv', S, a_t)
        S = S * w_t[:, :, None, :] + sa[..., None] * b_t[:, :, None, :] + v_t[..., None] * k_t[:, :, None, :]
        return S, jnp.einsum('bhvk,bhk->bhv', S, r_t)

    S, ys = lax.scan(step, S0, xs)
    return S, jnp.moveaxis(ys, 0, 1)


def rwkv_time_mix(xn, wkv0, shift0, mu, w_in, w0, w1, w2, a0, a1, a2, k_k, k_a, r_k,
                  ln_g, ln_b, w_out):
    B, T, _ = xn.shape
    f32 = jnp.float32
    xprev = jnp.concatenate([shift0[:, None].astype(xn.dtype), xn[:, :-1]], axis=1)
    xmix = xn[None] + (xprev - xn)[None] * mu[:, None, None, :].astype(xn.dtype)
    proj = jnp.einsum('cbtd,cdw->cbtw', xmix[:4], w_in)
    r, k, v, g = proj[0], proj[1], proj[2], proj[3]
    xw, xa = xmix[4], xmix[5]
    z = (w0 + jnp.tanh(xw @ w1) @ w2).astype(f32)
    log_w = -jax.nn.softplus(-z) - 0.5
    decay = jnp.exp(-jnp.exp(log_w))
    a = jax.nn.sigmoid((a0 + (xa @ a1) @ a2).astype(f32))

    def hd(t):
        return t.reshape(B, T, R_HEADS, R_HEAD_DIM)

    kk = hd((k * k_k).astype(f32))
    kk = kk / jnp.maximum(jnp.sqrt(jnp.sum(kk * kk, axis=-1, keepdims=True)), 1e-12)
    k_h = hd((k.astype(f32) * (1.0 + (a - 1.0) * k_a.astype(f32))))
    r_h, v_h, a_h, w_h = hd(r.astype(f32)), hd(v.astype(f32)), hd(a), hd(decay)
    S, y = wkv_scan(wkv0.astype(f32), r_h, w_h, k_h, v_h, -kk, kk * a_h)
    mean = jnp.mean(y, axis=-1, keepdims=True)
    var = jnp.mean(jnp.square(y - mean), axis=-1, keepdims=True)
    y = ((y - mean) * lax.rsqrt(var + GN_EPS)).reshape(B, T, R_WIDTH)
    y = y * ln_g.astype(f32) + ln_b.astype(f32)
    bonus = jnp.sum(r_h * k_h * r_k.astype(f32), axis=-1, keepdims=True) * v_h
    y = (y + bonus.reshape(B, T, R_WIDTH)) * jax.nn.silu(g.astype(f32))
    out = jnp.einsum('btw,wd->btd', y.astype(xn.dtype), w_out)
    return out, S, xn[:, -1]


def run_trunk(h, p, win_k, win_v, wkv, shift, is_prompt):
    new_k, new_v, new_wkv, new_shift = [], [], [], []
    for i in range(DEPTH):
        j = i // N_MIXERS
        xn = rmsnorm(h, p['norm_gain'][i])
        if i % N_MIXERS == 0:
            if is_prompt:
                y, ck, cv = attn_prompt(xn, p['attn_w_in'][j], p['attn_sinks'][j], p['attn_w_out'][j], p['table'])
            else:
                y, ck, cv = attn_sample(xn, win_k[j], win_v[j], p['attn_w_in'][j], p['attn_sinks'][j],
                                        p['attn_w_out'][j], p['table'])
            new_k.append(ck)
            new_v.append(cv)
        else:
            y, s, sh = rwkv_time_mix(xn, wkv[j], shift[j], p['mu'][j], p['r_w_in'][j], p['w0'][j], p['w1'][j],
                                     p['w2'][j], p['a0'][j], p['a1'][j], p['a2'][j], p['k_k'][j], p['k_a'][j],
                                     p['r_k'][j], p['ln_g'][j], p['ln_b'][j], p['r_w_out'][j])
            new_wkv.append(s)
            new_shift.append(sh)
        h = h + y
    return (rmsnorm(h, p['final_gain']), jnp.stack(new_k), jnp.stack(new_v),
            jnp.stack(new_wkv), jnp.stack(new_shift))


def setup_inputs(seed: int = 0) -> dict:
    key = jax.random.key(seed)
    ks = jax.random.split(key, 32)
    nrm = jax.random.normal
    f32 = jnp.float32
    NA, NR = N_ATTN_LAYERS, N_RWKV_LAYERS
    return {
        "x_prompt": nrm(ks[0], (BATCH, SEQ, D_MODEL), f32),
        "x_sample": nrm(ks[1], (DEC_BATCH, DEC_SEQ, D_MODEL), f32),
        "cache_win_k": nrm(ks[2], (NA, DEC_BATCH, min(WINDOW, PAST_LEN), A_KV_HEADS, A_HEAD_DIM), f32),
        "cache_win_v": nrm(ks[3], (NA, DEC_BATCH, min(WINDOW, PAST_LEN), A_KV_HEADS, A_HEAD_DIM), f32),
        "state_wkv": 0.3 * nrm(ks[4], (NR, DEC_BATCH, R_HEADS, R_HEAD_DIM, R_HEAD_DIM), f32),
        "state_shift": nrm(ks[5], (NR, DEC_BATCH, D_MODEL), f32),
        "meta_tokens": nrm(ks[6], (N_META, D_MODEL), f32),
        "rel_bias_table": 0.5 * nrm(ks[7], (N_BUCKETS, A_HEADS), f32),
        "norm_gain": 1.0 + 0.05 * nrm(ks[8], (DEPTH, D_MODEL), f32),
        "final_gain": 1.0 + 0.05 * nrm(ks[9], (D_MODEL,), f32),
        "attn_w_in": nrm(ks[10], (NA, D_MODEL, 2 * A_WIDTH + 2 * A_KV_WIDTH), f32) * D_MODEL ** -0.5,
        "attn_sinks": 0.5 * nrm(ks[11], (NA, A_HEADS), f32),
        "attn_w_out": nrm(ks[12], (NA, A_WIDTH, D_MODEL), f32) * A_WIDTH ** -0.5,
        "rwkv_mu": jax.random.uniform(ks[13], (NR, N_SHIFT_MIX, D_MODEL), f32),
        "rwkv_w_in": nrm(ks[14], (NR, 4, D_MODEL, R_WIDTH), f32) * D_MODEL ** -0.5,
        "rwkv_w0": -1.0 + 0.5 * nrm(ks[15], (NR, R_WIDTH), f32),
        "rwkv_w1": nrm(ks[16], (NR, D_MODEL, DECAY_LORA), f32) * D_MODEL ** -0.5,
        "rwkv_w2": 0.5 * nrm(ks[17], (NR, DECAY_LORA, R_WIDTH), f32) * DECAY_LORA ** -0.5,
        "rwkv_a0": 0.5 * nrm(ks[18], (NR, R_WIDTH), f32),
        "rwkv_a1": nrm(ks[19], (NR, D_MODEL, ICLR_LORA), f32) * D_MODEL ** -0.5,
        "rwkv_a2": 0.5 * nrm(ks[20], (NR, ICLR_LORA, R_WIDTH), f32) * ICLR_LORA ** -0.5,
        "rwkv_k_k": 0.85 + 0.05 * nrm(ks[21], (NR, R_WIDTH), f32),
        "rwkv_k_a": 1.0 + 0.05 * nrm(ks[22], (NR, R_WIDTH), f32),
        "rwkv_r_k": 0.1 * nrm(ks[23], (NR, R_HEADS, R_HEAD_DIM), f32),
        "rwkv_ln_gamma": 1.0 + 0.05 * nrm(ks[24], (NR, R_WIDTH), f32),
        "rwkv_ln_beta": 0.02 * nrm(ks[25], (NR, R_WIDTH), f32),
        "rwkv_w_out": nrm(ks[26], (NR, R_WIDTH, D_MODEL), f32) * R_WIDTH ** -0.5,
    }


def reference(x_prompt, x_sample, cache_win_k, cache_win_v, state_wkv, state_shift,
              meta_tokens, rel_bias_table, norm_gain, final_gain,
              attn_w_in, attn_sinks, attn_w_out,
              rwkv_mu, rwkv_w_in, rwkv_w0, rwkv_w1, rwkv_w2, rwkv_a0, rwkv_a1, rwkv_a2,
              rwkv_k_k, rwkv_k_a, rwkv_r_k, rwkv_ln_gamma, rwkv_ln_beta, rwkv_w_out):
    p = dict(table=rel_bias_table, norm_gain=norm_gain, final_gain=final_gain,
             attn_w_in=attn_w_in, attn_sinks=attn_sinks, attn_w_out=attn_w_out,
             mu=rwkv_mu, r_w_in=rwkv_w_in, w0=rwkv_w0, w1=rwkv_w1, w2=rwkv_w2,
             a0=rwkv_a0, a1=rwkv_a1, a2=rwkv_a2, k_k=rwkv_k_k, k_a=rwkv_k_a, r_k=rwkv_r_k,
             ln_g=rwkv_ln_gamma, ln_b=rwkv_ln_beta, r_w_out=rwkv_w_out)
    B = x_prompt.shape[0]
    meta = jnp.broadcast_to(meta_tokens[None].astype(x_prompt.dtype), (B, N_META, D_MODEL))
    h_prompt = jnp.concatenate([meta, x_prompt], axis=1)
    wkv_zero = jnp.zeros((N_RWKV_LAYERS, B, R_HEADS, R_HEAD_DIM, R_HEAD_DIM), jnp.float32)
    shift_zero = jnp.zeros((N_RWKV_LAYERS, B, D_MODEL), x_prompt.dtype)
    out_p, new_win_k_prompt, new_win_v_prompt, new_wkv_prompt, new_shift_prompt = run_trunk(
        h_prompt, p, None, None, wkv_zero, shift_zero, True)
    y_prompt = out_p[:, N_META:]
    y_sample, new_win_k_sample, new_win_v_sample, new_wkv_sample, new_shift_sample = run_trunk(
        x_sample, p, cache_win_k, cache_win_v, state_wkv, state_shift, False)
    return (y_prompt, y_sample,
            new_win_k_prompt, new_win_v_prompt, new_wkv_prompt, new_shift_prompt,
            new_win_k_sample, new_win_v_sample, new_wkv_sample, new_shift_sample)
```

```python
import math
from contextlib import ExitStack
import numpy as np
import concourse.bass as bass
import concourse.mybir as mybir
from concourse.bass_utils import run_bass_kernel_spmd

F32 = mybir.dt.float32
AF = mybir.ActivationFunctionType
ALU = mybir.AluOpType
AX = mybir.AxisListType

NT = 33
NEG = -30000.0
CDEC = math.exp(-0.5)
NSEQ = 16
NTOK = 64


class Prog:
    EPOCH = 12000

    def __init__(self, nc, st):
        self.nc, self.st = nc, st
        self.eng = dict(pe=nc.tensor, act=nc.scalar, dve=nc.vector, pool=nc.gpsimd, sp=nc.sync)
        self.cnt = {e: 0 for e in self.eng}
        self.sems = {e: [] for e in self.eng}
        self.seen = {e: {} for e in self.eng}
        self.lastw, self.readers, self.streams = {}, {}, {}
        self.nwaits = 0

    def _sem(self, e, t):
        ep = (t - 1) // self.EPOCH
        while len(self.sems[e]) <= ep:
            self.sems[e].append(self.st.enter_context(self.nc.semaphore(f"s_{e}_{len(self.sems[e])}")))
        return self.sems[e][ep], t - ep * self.EPOCH

    def _wait(self, e, src, t):
        if t <= self.seen[e].get(src, 0):
            return
        if src in self.eng:
            sem, v = self._sem(src, t)
            self.eng[e].wait_ge(sem, v)
        else:
            self.eng[e].wait_ge(self.streams[src][0], 16 * t)
        self.nwaits += 1
        self.seen[e][src] = t

    def _deps(self, e, reads, writes, is_dma):
        need = {}

        def add(s, t, kind):
            if s == e and not is_dma:
                if e == 'pe' or kind == 'war':
                    return
            if need.get(s, 0) < t:
                need[s] = t
        for b in reads:
            w = self.lastw.get(b)
            if w:
                add(w[0], w[1], 'raw')
        for b in writes:
            w = self.lastw.get(b)
            if w:
                add(w[0], w[1], 'waw')
            for s, t in self.readers.get(b, {}).items():
                add(s, t, 'war')
        for s, t in need.items():
            self._wait(e, s, t)

    def _commit(self, tk, reads, writes):
        for b in reads:
            self.readers.setdefault(b, {})[tk[0]] = tk[1]
        for b in writes:
            self.lastw[b] = tk
            self.readers[b] = {}

    def op(self, e, fn, reads=(), writes=()):
        self._deps(e, reads, writes, False)
        inst = fn(self.eng[e])
        self.cnt[e] += 1
        t = self.cnt[e]
        inst.then_inc(self._sem(e, t)[0], 1)
        self._commit((e, t), reads, writes)

    def dma(self, q, stream, out, in_, reads=(), writes=()):
        self._deps(q, reads, writes, True)
        if stream not in self.streams:
            self.streams[stream] = [self.st.enter_context(self.nc.semaphore(f"d_{stream}")), 0]
        s = self.streams[stream]
        inst = self.eng[q].dma_start(out=out, in_=in_)
        s[1] += 1
        inst.then_inc(s[0], 16)
        self._commit((stream, s[1]), reads, writes)

    def finish(self, e='pool'):
        for name, (sem, n) in self.streams.items():
            self._wait(e, name, n)


def t5_bucket(rel):
    n = max(rel, 0)
    if n < 16:
        return n
    nf = np.float32(max(n, 16))
    large = 16 + int(np.float32(np.log(nf / np.float32(16)) / np.float32(math.log(128 / 16))) * np.float32(16))
    return min(large, 31)


def host_consts():
    ident = np.eye(128, dtype=np.float32)
    idx = np.arange(128)
    Us = (idx[:, None] < idx[None, :]).astype(np.float32)
    Ui = (idx[:, None] <= idx[None, :]).astype(np.float32)
    Ls = Us.T.copy()
    e127 = np.zeros((128, 1), np.float32)
    e127[127] = 1
    cm = np.zeros((2, 128, 256), np.float32)
    cm[0][:, :240] = NEG
    cm[1][:, :112] = NEG
    BD = np.zeros((128, 64), np.float32)
    for j in range(64):
        for i in range(64):
            if j // 4 == i // 4 and j <= i:
                BD[j, i] = 1
    cst = np.concatenate([ident, Us, Ui, Us, Ui, Ls, e127, cm[0], cm[1], BD], axis=1)
    E = np.zeros((33, 383), np.float32)
    for i in range(383):
        rel = i - 127
        if 0 <= rel < 128:
            E[t5_bucket(rel), i] = 1
        else:
            E[32, i] = 1
    return np.ascontiguousarray(cst), E


C_ID, C_M4, C_LS, C_E127, C_CM0, C_CM1 = 0, 128, 640, 768, 769, 1025
C_BD = 1281
CSTW = 1345


def build(stage=9):
    nc = bass.Bass("TRN2", target_bir_lowering=False)
    st = ExitStack()
    P = Prog(nc, st)

    def din(n, s):
        return nc.dram_tensor(n, list(s), F32, kind="ExternalInput").ap()

    def dout(n, s):
        return nc.dram_tensor(n, list(s), F32, kind="ExternalOutput").ap()

    def T(n, s):
        return st.enter_context(nc.sbuf_tensor("sb_" + n, list(s), F32))

    xp = din("xp", [NT * 128, 1024])
    a_win = din("a_win", [1024, 2560])
    a_wout = din("a_wout", [1024, 1024])
    r_win = din("r_win", [4, 1024, 1024])
    r_wout = din("r_wout", [1024, 1024])
    d_w1 = din("w1", [1024, 64])
    d_a1 = din("a1", [1024, 64])
    d_w2 = din("w2w0", [65, 1024])
    d_a2 = din("a2a0", [65, 1024])
    d_vec = din("vecs", [8, 1024])
    d_mu = din("mu", [6, 1024])
    d_tab = din("tab33", [33, 16])
    d_sink = din("sinks", [1, 16])
    d_cst = din("cst", [128, CSTW])
    d_E = din("E", [33, 383])
    yp = dout("yp", [4096, 1024])
    o_wk = dout("o_wk", [128, 256])
    o_wv = dout("o_wv", [128, 256])
    o_wkv = dout("o_wkv", [16, 64, 64])
    o_shp = dout("o_shp", [1, 1024])

    cst = T("cst", [128, CSTW])
    ident = cst[:, C_ID:C_ID + 128]
    bias = T("bias", [128, 16, 256])
    sinkbc = T("sinkbc", [128, 16])
    vbc = {n: T("v_" + n, [128, 1024]) for n in ["g1", "gf", "k_k", "k_a", "r_k", "ln_g", "ln_b"]}
    VIDX = dict(g0=0, g1=1, gf=2, k_k=3, k_a=4, r_k=5, ln_g=6, ln_b=7)
    g0T = T("g0T", [128, 8])
    wp = [T(f"wp{i}", [128, 1024]) for i in range(4)]
    hb = [T("hA", [128, 1024]), T("hB", [128, 1024])]
    xn = T("xn", [128, 1024])
    xnT = T("xnT", [128, 8, 128])
    q_tok = T("q_tok", [128, 1024])
    kv_tok = T("kv_tok", [128, 512])
    sg = T("sg", [128, 1024])
    qT = T("qT", [64, 16, 128])
    kT = T("kT", [64, 4, 256])
    vprev = T("vprev", [128, 256])
    S_sb = T("S_sb", [128, 4, 256])
    S_sb1 = T("S_sb1", [128, 4, 256])
    PTs = T("PTs", [128, 8, 128])
    og = T("og", [128, 1024])
    ogT = T("ogT", [128, 8, 128])
    small = T("small", [128, 64])
    junk = T("junk", [128, 1024])
    PB = [st.enter_context(nc.psum_tensor(f"PB{i}", [128, 512], F32)) for i in range(8)]
    pb = [f"PB{i}" for i in range(8)]

    ss, ms, rstd, rstd8 = small[:, 0:1], small[:, 1:2], small[:, 2:3], small[:, 3:4]
    mx, negm, rs, es, den = small[:, 4:8], small[:, 8:12], small[:, 12:16], small[:, 16:20], small[:, 20:24]

    P.dma('sp', 'c_cst', cst[:], d_cst[:, :], writes=['cst'])
    for n, tl in vbc.items():
        src = bass.AP(d_vec.tensor, VIDX[n] * 1024, [[0, 128], [1, 1024]])
        P.dma('sp', 'c_' + n, tl[:], src, writes=['v_' + n])
    P.dma('sp', 'c_sink', sinkbc[:], bass.AP(d_sink.tensor, 0, [[0, 128], [1, 16]]), writes=['sinkbc'])
    g0r = T("g0r", [8, 128])
    P.dma('sp', 'c_g0r', g0r[:], bass.AP(d_vec.tensor, 0, [[128, 8], [1, 128]]), writes=['g0r'])
    P.op('pe', lambda e: e.transpose(PB[0][:, 0:8], g0r[:, :], cst[0:8, C_ID:C_ID + 8]), reads=['g0r', 'cst'], writes=[pb[0]])
    P.op('act', lambda e: e.copy(out=g0T[:], in_=PB[0][:, 0:8]), reads=[pb[0]], writes=['g0T'])
    Esb = T("Esb", [33, 383])
    tab = T("tab", [33, 16])
    P.dma('sp', 'c_E', Esb[:], d_E[:, :], writes=['Esb'])
    P.dma('sp', 'c_tab', tab[:], d_tab[:, :], writes=['tab'])
    for r8 in range(8):
        bk = r8 % 2
        for sl in range(32):
            s = r8 * 32 + sl
            P.op('pe', lambda e, s=s, sl=sl, bk=bk: e.matmul(PB[bk][:, sl * 16:(sl + 1) * 16], lhsT=Esb[:, 255 - s:255 - s + 128],
                                                           rhs=tab[:, :], start=True, stop=True),
                 reads=['Esb', 'tab'], writes=[pb[bk]])
        P.op('act', lambda e, r8=r8, bk=bk: e.copy(out=bias[:, :, r8 * 32:(r8 + 1) * 32].rearrange("q h s -> q s h"),
                                                  in_=PB[bk][:, :].rearrange("q (s h) -> q s h", h=16)),
             reads=[pb[bk]], writes=['bias'])
    P.op('pool', lambda e: e.memset(kT[:], 0.0), writes=['kT'])
    P.op('pool', lambda e: e.memset(vprev[:], 0.0), writes=['vprev'])

    slot_i = [0]

    def project(xT, xTname, wsrc2d, c0, banks, nrow=128, ncol=1024):
        nb = (ncol + 511) // 512
        for kc in range(8):
            i = slot_i[0] % 4
            slot_i[0] += 1
            P.dma('sp', f'wp{i}', wp[i][:, 0:ncol], wsrc2d[kc * 128:(kc + 1) * 128, c0:c0 + ncol], writes=[f'wp{i}'])
            for b in range(nb):
                w = min(512, ncol - b * 512)
                P.op('pe', lambda e, i=i, b=b, w=w, kc=kc: e.matmul(PB[banks[b]][:nrow, :w], lhsT=xT[:, kc, :nrow], rhs=wp[i][:, b * 512:b * 512 + w],
                                                                  start=(kc == 0), stop=(kc == 7)),
                     reads=[xTname, f'wp{i}'], writes=[pb[banks[b]]])

    def rms_rstd(src, srcname, nrow=128):
        P.op('act', lambda e: e.activation(out=junk[:nrow, :], in_=src[:nrow, :], func=AF.Square, accum_out=ss[:nrow, :]),
             reads=[srcname], writes=['junk', 'ss'])
        P.op('dve', lambda e: e.tensor_scalar(out=ms[:nrow, :], in0=ss[:nrow, :], scalar1=1.0 / 1024, scalar2=1e-6, op0=ALU.mult, op1=ALU.add),
             reads=['ss'], writes=['ms'])
        P.op('act', lambda e: e.activation(out=ms[:nrow, :], in_=ms[:nrow, :], func=AF.Sqrt), reads=['ms'], writes=['ms'])
        P.op('dve', lambda e: e.reciprocal(out=rstd[:nrow, :], in_=ms[:nrow, :]), reads=['ms'], writes=['rstd'])

    def transpose_to(src, srcname, dst, dstname, banks, nrow=128, scale_cols=None):
        for c in range(8):
            bk = banks[c // 4]
            P.op('pe', lambda e, c=c, bk=bk: e.transpose(PB[bk][:, (c % 4) * 128:(c % 4) * 128 + nrow], src[:nrow, c * 128:(c + 1) * 128],
                                                        ident[:nrow, :nrow]),
                 reads=[srcname, 'cst'], writes=[pb[bk]])
        for half in range(2):
            bk = banks[half]
            if scale_cols is None:
                P.op('act', lambda e, half=half, bk=bk: e.copy(out=dst[:, half * 4:(half + 1) * 4, :nrow],
                                                            in_=PB[bk][:, :].rearrange("p (c t) -> p c t", c=4)[:, :, :nrow]),
                     reads=[pb[bk]], writes=[dstname])
            else:
                P.op('dve', lambda e, half=half, bk=bk: e.tensor_tensor(
                    out=dst[:, half * 4:(half + 1) * 4, :nrow], in0=PB[bk][:, :].rearrange("p (c t) -> p c t", c=4)[:, :, :nrow],
                    in1=scale_cols[:, half * 4:(half + 1) * 4].unsqueeze(2).to_broadcast([128, 4, nrow]), op=ALU.mult),
                    reads=[pb[bk], 'g0T'], writes=[dstname])

    def attention(nq, kT_ap_fn, nk, vblocks, cm_ap, O_banks):
        nb = len(vblocks)

        def scores(g):
            for j in range(4):
                hh = 4 * g + j
                bk = j // 2
                P.op('pe', lambda e, hh=hh, j=j, bk=bk, g=g: e.matmul(PB[bk][:nq, (j % 2) * 256:(j % 2) * 256 + nk], lhsT=qT[:, hh, :nq],
                                                                   rhs=kT_ap_fn(g), start=True, stop=True),
                     reads=['qT', 'kT'], writes=[pb[bk]])

        def softmax(g, S, sn):
            for bk in range(2):
                P.op('dve', lambda e, bk=bk, g=g: e.tensor_tensor(
                    out=S[:nq, 2 * bk:2 * bk + 2, :nk], in0=PB[bk][:nq, :].rearrange("q (a s) -> q a s", a=2)[:, :, :nk],
                    in1=bias[:nq, 4 * g + 2 * bk:4 * g + 2 * bk + 2, :nk], op=ALU.add),
                    reads=[pb[bk], 'bias'], writes=[sn])
            if cm_ap is not None:
                P.op('dve', lambda e: e.tensor_tensor(out=S[:nq, :, :nk], in0=S[:nq, :, :nk],
                                                      in1=cm_ap[:nq, :nk].unsqueeze(1).to_broadcast([nq, 4, nk]), op=ALU.add),
                     reads=[sn, 'cst'], writes=[sn])
            P.op('dve', lambda e: e.tensor_reduce(out=mx[:nq, :], in_=S[:nq, :, :nk], axis=AX.X, op=ALU.max), reads=[sn], writes=['mx'])
            P.op('dve', lambda e, g=g: e.tensor_tensor(out=mx[:nq, :], in0=mx[:nq, :], in1=sinkbc[:nq, 4 * g:4 * g + 4], op=ALU.max),
                 reads=['mx', 'sinkbc'], writes=['mx'])
            P.op('dve', lambda e: e.tensor_scalar(out=negm[:nq, :], in0=mx[:nq, :], scalar1=-1.0, scalar2=None, op0=ALU.mult),
                 reads=['mx'], writes=['negm'])
            P.op('dve', lambda e, g=g: e.tensor_tensor(out=es[:nq, :], in0=sinkbc[:nq, 4 * g:4 * g + 4], in1=negm[:nq, :], op=ALU.add),
                 reads=['negm', 'sinkbc'], writes=['es'])
            for j in range(4):
                P.op('act', lambda e, j=j: e.activation(out=S[:nq, j, :nk], in_=S[:nq, j, :nk], func=AF.Exp, bias=negm[:nq, j:j + 1],
                                                        accum_out=rs[:nq, j:j + 1]),
                     reads=[sn, 'negm'], writes=[sn, 'rs'])
            P.op('act', lambda e: e.activation(out=es[:nq, :], in_=es[:nq, :], func=AF.Exp), reads=['es'], writes=['es'])
            P.op('dve', lambda e: e.tensor_tensor(out=den[:nq, :], in0=rs[:nq, :], in1=es[:nq, :], op=ALU.add), reads=['rs', 'es'], writes=['den'])
            P.op('dve', lambda e: e.reciprocal(out=den[:nq, :], in_=den[:nq, :]), reads=['den'], writes=['den'])
            P.op('dve', lambda e: e.tensor_tensor(out=S[:nq, :, :nk], in0=S[:nq, :, :nk],
                                                  in1=den[:nq, :].unsqueeze(2).to_broadcast([nq, 4, nk]), op=ALU.mult),
                 reads=[sn, 'den'], writes=[sn])

        def pv(g, S, sn):
            for j in range(4):
                for b in range(nb):
                    rows = vblocks[b][1]
                    idx = j * 2 + b
                    bk = 2 + idx // 4
                    P.op('pe', lambda e, j=j, b=b, rows=rows, idx=idx, bk=bk: e.transpose(
                        PB[bk][:rows, (idx % 4) * 128:(idx % 4) * 128 + nq], S[:nq, j, b * 128:b * 128 + rows], ident[:nq, :nq]),
                        reads=[sn, 'cst'], writes=[pb[bk]])
            for half in range(2):
                bk = 2 + half
                P.op('act', lambda e, half=half, bk=bk: e.copy(out=PTs[:, half * 4:(half + 1) * 4, :nq],
                                                            in_=PB[bk][:, :].rearrange("p (c t) -> p c t", c=4)[:, :, :nq]),
                     reads=[pb[bk]], writes=['PTs'])
            for j in range(4):
                hh = 4 * g + j
                ob = O_banks[hh // 8]
                for b in range(nb):
                    vfn, rows, vname = vblocks[b]
                    P.op('pe', lambda e, j=j, b=b, hh=hh, ob=ob, vfn=vfn, rows=rows, g=g: e.matmul(
                        PB[ob][:nq, (hh % 8) * 64:(hh % 8) * 64 + 64], lhsT=PTs[:rows, j * 2 + b, :nq], rhs=vfn(g),
                        start=(b == 0), stop=(b == nb - 1)),
                        reads=['PTs', vname], writes=[pb[ob]])

        Sb = [(S_sb, 'S_sb'), (S_sb1, 'S_sb1')]
        scores(0)
        for g in range(4):
            S, sn = Sb[g % 2]
            softmax(g, S, sn)
            if g + 1 < 4:
                scores(g + 1)
            pv(g, S, sn)

    d_xs = din("xs", [NTOK, 1024])
    d_ck = din("ck", [NSEQ, 128, 256])
    d_cv = din("cv", [NSEQ, 128, 256])
    d_swkv = din("swkv", [NSEQ, 16, 64, 64])
    d_ssh = din("ssh", [NSEQ, 1024])
    ys = dout("ys", [NTOK, 1024])
    o_wks = dout("o_wks", [NSEQ, 128, 256])
    o_wvs = dout("o_wvs", [NSEQ, 128, 256])
    o_wkvs = dout("o_wkvs", [NSEQ, 16, 64, 64])
    o_shs = dout("o_shs", [NSEQ, 1024])

    xnT1 = T("xnT1", [128, 8, 129])
    xm = [T("xm0", [128, 8, 128]), T("xm1", [128, 8, 128])]
    k_tok = T("k_tok", [128, 1024])
    v_tok = T("v_tok", [128, 1024])
    ART = T("ART", [64, 8, 2, 128])
    KT = T("KT", [64, 8, 128])
    Hst = T("Hst", [64, 16, 64])
    gC = T("gC", [64, 16])
    ASall = T("ASall", [128, 4, 512])
    AS = [ASall[:, i, :] for i in range(4)]
    QP = [T("QPa", [128, 1024]), T("QPb", [128, 1024])]
    XS = [T("XSa", [128, 512]), T("XSb", [128, 512])]
    Wsb = T("Wsb", [128, 256])
    Usb = T("Usb", [128, 256])
    w1s = T("w1s", [128, 8, 64])
    a1s = T("a1s", [128, 8, 64])
    w2s = T("w2s", [65, 1024])
    a2s = T("a2s", [65, 1024])
    tw = T("tw", [65, 128])
    ta = T("ta", [65, 128])
    muT = T("muT", [128, 48])
    mur = T("mur", [48, 128])
    s16 = T("s16", [128, 64])
    ck_tok = T("ck_tok", [128, 256])
    shr = T("shr", [8, 128])
    Sld = ASall[0:64, 0:2, :].rearrange("p a b -> p (a b)").rearrange("p (h k) -> p h k", h=16)
    r_tok, sgl, sgz, asg = q_tok, sg, og, ogT[:, :, :].rearrange("p c t -> p (c t)")
    kkb = S_sb[:, :, :].rearrange("p a s -> p (a s)")
    tmpA = PTs[:, :, :].rearrange("p c t -> p (c t)")
    ysb = junk
    dxT = xnT
    gam = xm[0][:, :, :].rearrange("p c t -> p (c t)")
    ginv = xm[1][:, :, :].rearrange("p c t -> p (c t)")
    BT = qT
    P.dma('sp', 'c_w1', w1s[:], d_w1.rearrange("(c p) n -> p c n", p=128), writes=['w1s'])
    P.dma('sp', 'c_a1', a1s[:], d_a1.rearrange("(c p) n -> p c n", p=128), writes=['a1s'])
    P.dma('sp', 'c_w2', w2s[:], d_w2[:, :], writes=['w2s'])
    P.dma('sp', 'c_a2', a2s[:], d_a2[:, :], writes=['a2s'])
    P.dma('sp', 'c_mur', mur[:], bass.AP(d_mu.tensor, 0, [[128, 48], [1, 128]]), writes=['mur'])
    P.op('pe', lambda e: e.transpose(PB[1][:, 0:48], mur[:, :], cst[0:48, C_ID:C_ID + 48]), reads=['mur', 'cst'], writes=[pb[1]])
    P.op('act', lambda e: e.copy(out=muT[:], in_=PB[1][:, 0:48]), reads=[pb[1]], writes=['muT'])
    P.op('pool', lambda e: e.memset(tw[:], 1.0), writes=['tw'])
    P.op('pool', lambda e: e.memset(ta[:], 1.0), writes=['ta'])
    P.op('pool', lambda e: e.memset(xnT1[:], 0.0), writes=['xnT1'])
    P.op('pool', lambda e: e.memset(Hst[:], 0.0), writes=['Hst'])
    v3 = lambda ap, R: ap[:R, :].rearrange("p (h d) -> p h d", h=16)
    bc16 = lambda ap, R: ap[:R, 0:16].unsqueeze(2).to_broadcast([R, 16, 64])
    M4 = cst[:, C_M4:C_M4 + 512]
    LS = cst[:, C_LS:C_LS + 128]
    UI = cst[:, C_M4 + 128:C_M4 + 256]
    BDm = cst[:, C_BD:C_BD + 64]
    xpT = T("xpT", [128, 8, 64])
    sc = {n: nc.dram_tensor("sc_" + n, [NTOK, w], F32).ap() for n, w in
          [('q', 1024), ('kv', 512), ('sg', 1024), ('og', 1024), ('At', 1024), ('Rt', 1024), ('Bt', 1024), ('Kt', 1024), ('v', 1024),
           ('gam', 1024), ('y', 1024)]}

    def l0_tile(R, xsrc, hn, H, cm_ap, sample_seq=None, last_prompt=False, parts='ABC'):
        if 'A' in parts:
            l0_A(R, xsrc, hn, H)
        if 'B' in parts:
            l0_B(R, cm_ap, sample_seq, last_prompt)
        if 'C' in parts:
            l0_C(R, hn, H)

    def l0_A(R, xsrc, hn, H):
        P.dma('sp', hn, H[:R, :], xsrc, writes=[hn])
        transpose_to(H, hn, xnT, 'xnT', (2, 3), nrow=R, scale_cols=g0T)
        rms_rstd(H, hn, R)
        P.op('dve', lambda e: e.tensor_scalar(out=rstd8[:R, :], in0=rstd[:R, :], scalar1=0.125, scalar2=None, op0=ALU.mult),
             reads=['rstd'], writes=['rstd8'])
        project(xnT, 'xnT', a_win, 0, (0, 1), nrow=R)
        for pi in range(2):
            P.op('act', lambda e, pi=pi: e.activation(out=q_tok[:R, pi * 512:(pi + 1) * 512], in_=PB[pi][:R, :], func=AF.Copy, scale=rstd8[:R, :]),
                 reads=[pb[pi], 'rstd8'], writes=['q_tok'])
        project(xnT, 'xnT', a_win, 1024, (2,), nrow=R, ncol=512)
        P.op('dve', lambda e: e.tensor_scalar(out=kv_tok[:R, :], in0=PB[2][:R, :], scalar1=rstd[:R, :], scalar2=None, op0=ALU.mult),
             reads=[pb[2], 'rstd'], writes=['kv_tok'])
        project(xnT, 'xnT', a_win, 1536, (3, 4), nrow=R)
        for pi in range(2):
            P.op('act', lambda e, pi=pi: e.activation(out=sg[:R, pi * 512:(pi + 1) * 512], in_=PB[3 + pi][:R, :], func=AF.Silu, scale=rstd[:R, :]),
                 reads=[pb[3 + pi], 'rstd'], writes=['sg'])

    def l0_B(R, cm_ap, sample_seq, last_prompt):
        for hh in range(16):
            bk = 4 + hh // 4
            P.op('pe', lambda e, hh=hh, bk=bk: e.transpose(PB[bk][0:64, (hh % 4) * 128:(hh % 4) * 128 + R], q_tok[:R, hh * 64:(hh + 1) * 64],
                                                        ident[:R, :R]),
                 reads=['q_tok', 'cst'], writes=[pb[bk]])
        for q4 in range(4):
            P.op('act', lambda e, q4=q4: e.copy(out=qT[:, q4 * 4:(q4 + 1) * 4, :R],
                                               in_=PB[4 + q4][0:64, :].rearrange("p (c t) -> p c t", c=4)[:, :, :R]),
                 reads=[pb[4 + q4]], writes=['qT'])
        if sample_seq is not None:
            s_ = sample_seq
            P.dma('sp', 'ck_tok', ck_tok[:], d_ck[s_, :, :], writes=['ck_tok'])
            P.dma('sp', 'vprev', vprev[:], d_cv[s_, :, :], writes=['vprev'])
            for kv in range(4):
                P.op('pe', lambda e, kv=kv: e.transpose(PB[3][0:64, kv * 128:(kv + 1) * 128], ck_tok[:, kv * 64:(kv + 1) * 64], ident),
                     reads=['ck_tok', 'cst'], writes=[pb[3]])
            P.op('act', lambda e: e.copy(out=kT[:, :, 0:128], in_=PB[3][0:64, :].rearrange("p (c t) -> p c t", c=4)), reads=[pb[3]], writes=['kT'])
            P.dma('pool', 'o_wks', o_wks[s_, 0:124, :], ck_tok[4:128, :], reads=['ck_tok'])
            P.dma('pool', 'o_wvs', o_wvs[s_, 0:124, :], vprev[4:128, :], reads=['vprev'])
            P.dma('pool', 'o_wks2', o_wks[s_, 124:128, :], kv_tok[0:4, 0:256], reads=['kv_tok'])
            P.dma('pool', 'o_wvs2', o_wvs[s_, 124:128, :], kv_tok[0:4, 256:512], reads=['kv_tok'])
        for kv in range(4):
            P.op('pe', lambda e, kv=kv: e.transpose(PB[2][0:64, kv * 128:kv * 128 + R], kv_tok[:R, kv * 64:(kv + 1) * 64], ident[:R, :R]),
                 reads=['kv_tok', 'cst'], writes=[pb[2]])
        P.op('act', lambda e: e.copy(out=kT[:, :, 128:128 + R], in_=PB[2][0:64, :].rearrange("p (c t) -> p c t", c=4)[:, :, :R]),
             reads=[pb[2]], writes=['kT'])
        attention(R, lambda g: kT[:, g, 0:128 + R], 128 + R,
                  [(lambda g: vprev[:, g * 64:(g + 1) * 64], 128, 'vprev'), (lambda g: kv_tok[:R, 256 + g * 64:256 + (g + 1) * 64], R, 'kv_tok')],
                  cm_ap, (4, 5))
        for half in range(2):
            P.op('dve', lambda e, half=half: e.tensor_tensor(out=og[:R, half * 512:(half + 1) * 512], in0=PB[4 + half][:R, :],
                                                            in1=sg[:R, half * 512:(half + 1) * 512], op=ALU.mult),
                 reads=[pb[4 + half], 'sg'], writes=['og'])
        if sample_seq is None:
            P.op('pool', lambda e: e.tensor_copy(out=kT[:, :, 0:128], in_=kT[:, :, 128:256]), reads=['kT'], writes=['kT'])
            P.op('pool', lambda e: e.tensor_copy(out=vprev[:], in_=kv_tok[:, 256:512]), reads=['kv_tok'], writes=['vprev'])
        if last_prompt:
            P.dma('pool', 'o_wk', o_wk[:, :], kv_tok[:, 0:256], reads=['kv_tok'])
            P.dma('pool', 'o_wv', o_wv[:, :], kv_tok[:, 256:512], reads=['kv_tok'])

    def l0_C(R, hn, H):
        transpose_to(og, 'og', ogT, 'ogT', (6, 7), nrow=R)
        project(ogT, 'ogT', a_wout, 0, (0, 1), nrow=R)
        for pi in range(2):
            P.op('dve', lambda e, pi=pi: e.tensor_tensor(out=H[:R, pi * 512:(pi + 1) * 512], in0=PB[pi][:R, :], in1=H[:R, pi * 512:(pi + 1) * 512],
                                                         op=ALU.add),
                 reads=[pb[pi], hn], writes=[hn])

    def TT(eng, out, in0, in1, op, reads, writes):
        P.op(eng, lambda e: e.tensor_tensor(out=out, in0=in0, in1=in1, op=op), reads=reads, writes=writes)

    def STT(out, in0, scalar, in1, op0, op1, reads, writes):
        P.op('dve', lambda e: e.scalar_tensor_tensor(out=out, in0=in0, scalar=scalar, in1=in1, op0=op0, op1=op1), reads=reads, writes=writes)

    def load_state(s_):
        P.dma('sp', 'Sld', Sld, d_swkv[s_].rearrange("h v k -> v h k"), writes=['Sld', 'AS0', 'AS1'])
        for hh in range(16):
            bk = 4 + hh // 8
            P.op('pe', lambda e, hh=hh, bk=bk: e.transpose(PB[bk][0:64, (hh % 8) * 64:(hh % 8) * 64 + 64], Sld[:, hh, :], ident[:64, :64]),
                 reads=['Sld', 'cst'], writes=[pb[bk]])
        for half in range(2):
            P.op('act', lambda e, half=half: e.copy(out=Hst[:, half * 8:(half + 1) * 8, :],
                                                   in_=PB[4 + half][0:64, :].rearrange("p (h v) -> p h v", h=8)),
                 reads=[pb[4 + half]], writes=['Hst'])

    def compute_gC(R, gsrc, gname):
        for hh in range(16):
            P.op('pe', lambda e, hh=hh: e.matmul(PB[7][0:64, hh:hh + 1], lhsT=gsrc[:R, hh * 64:(hh + 1) * 64], rhs=ident[:R, R - 1:R], start=True, stop=True),
                 reads=[gname, 'cst'], writes=[pb[7]])
        P.op('act', lambda e: e.copy(out=gC[:, :], in_=PB[7][0:64, 0:16]), reads=[pb[7]], writes=['gC'])

    def l1_tile(R, hn, H, sample_seq=None, last_prompt=False, yout=None, parts='ABC', batched=False):
        if 'A' in parts:
            l1_A(R, hn, H, last_prompt, batched)
        if 'B' in parts:
            l1_B(R, sample_seq, last_prompt)
        if 'C' in parts:
            l1_C(R, hn, H, yout)

    def l1_A(R, hn, H, last_prompt, batched):
        rms_rstd(H, hn, R)
        STT(xn[:R, :], H[:R, :], rstd[:R, :], vbc['g1'][:R, :], ALU.mult, ALU.mult, [hn, 'rstd', 'v_g1'], ['xn'])
        if not batched:
            P.op('pool', lambda e: e.tensor_copy(out=xnT1[:, :, 0:1], in_=xnT1[:, :, 128:129]), reads=['xnT1'], writes=['xnT1'])
            if last_prompt:
                P.dma('pool', 'o_shp', o_shp[:, :], xn[127:128, :], reads=['xn'])
            transpose_to(xn, 'xn', xnT1[:, :, 1:129], 'xnT1', (2, 3), nrow=R)
            TT('pool', dxT[:, :, :R], xnT1[:, :, 0:R], xnT1[:, :, 1:1 + R], ALU.subtract, ['xnT1'], ['xnT'])
        else:
            xp4 = xpT[:, :, :].rearrange("p c (s t) -> p c s t", t=4)
            P.dma('sp', 'k_tok', k_tok[0:16, :], d_ssh[:, :], writes=['k_tok'])
            for c in range(8):
                P.op('pe', lambda e, c=c: e.transpose(PB[0][:, c * 16:(c + 1) * 16], k_tok[0:16, c * 128:(c + 1) * 128], ident[:16, :16]),
                     reads=['k_tok', 'cst'], writes=[pb[0]])
            P.op('act', lambda e: e.copy(out=xp4[:, :, :, 0], in_=PB[0][:, 0:128].rearrange("p (c s) -> p c s", c=8)), reads=[pb[0]], writes=['xpT'])
            for s_ in range(NSEQ):
                P.dma('pool', 'o_shs', o_shs[s_:s_ + 1, :], xn[4 * s_ + 3:4 * s_ + 4, :], reads=['xn'])
            transpose_to(xn, 'xn', xnT1[:, :, 1:129], 'xnT1', (2, 3), nrow=R)
            x14 = xnT1[:, :, 1:65].rearrange("p c (s t) -> p c s t", t=4)
            P.op('pool', lambda e: e.tensor_copy(out=xp4[:, :, :, 1:4], in_=x14[:, :, :, 0:3]), reads=['xnT1'], writes=['xpT'])
            TT('pool', dxT[:, :, :R], xpT[:, :, :R], xnT1[:, :, 1:1 + R], ALU.subtract, ['xnT1', 'xpT'], ['xnT'])
        dsts = [r_tok, k_tok, v_tok, sgl]
        dnames = ['q_tok', 'k_tok', 'v_tok', 'sg']
        def emit_kk():
            TT('dve', kkb[:R, :], k_tok[:R, :], vbc['k_k'][:R, :], ALU.mult, ['k_tok', 'v_k_k'], ['S_sb'])
            TT('dve', tmpA[:R, :], kkb[:R, :], kkb[:R, :], ALU.mult, ['S_sb'], ['PTs'])
            P.op('dve', lambda e: e.tensor_reduce(out=s16[:R, 0:16], in_=v3(tmpA, R), axis=AX.X, op=ALU.add), reads=['PTs'], writes=['s16'])
            P.op('act', lambda e: e.activation(out=s16[:R, 0:16], in_=s16[:R, 0:16], func=AF.Sqrt), reads=['s16'], writes=['s16'])
            P.op('dve', lambda e: e.tensor_scalar(out=s16[:R, 0:16], in0=s16[:R, 0:16], scalar1=1e-12, scalar2=None, op0=ALU.max), reads=['s16'], writes=['s16'])
            P.op('dve', lambda e: e.reciprocal(out=s16[:R, 0:16], in_=s16[:R, 0:16]), reads=['s16'], writes=['s16'])
            TT('dve', v3(kkb, R), v3(kkb, R), bc16(s16, R), ALU.mult, ['S_sb', 's16'], ['S_sb'])

        def emit_kh():
            STT(tmpA[:R, :], asg[:R, :], -1.0, vbc['k_a'][:R, :], ALU.add, ALU.mult, ['ogT', 'v_k_a'], ['PTs'])
            STT(k_tok[:R, :], tmpA[:R, :], 1.0, k_tok[:R, :], ALU.add, ALU.mult, ['PTs', 'k_tok'], ['k_tok'])
            TT('dve', tmpA[:R, :], r_tok[:R, :], k_tok[:R, :], ALU.mult, ['q_tok', 'k_tok'], ['PTs'])
            TT('dve', tmpA[:R, :], tmpA[:R, :], vbc['r_k'][:R, :], ALU.mult, ['PTs', 'v_r_k'], ['PTs'])
            P.op('dve', lambda e: e.tensor_reduce(out=s16[:R, 16:32], in_=v3(tmpA, R), axis=AX.X, op=ALU.add), reads=['PTs'], writes=['s16b'])
            TT('dve', v3(tmpA, R), v3(v_tok, R), s16[:R, 16:32].unsqueeze(2).to_broadcast([R, 16, 64]), ALU.mult, ['v_tok', 's16b'], ['PTs'])

        for c in range(6):
            X = xm[c % 2]
            xname = f'xm{c % 2}'
            TT('pool', X[:, :, :R], dxT[:, :, :R], muT[:, c * 8:(c + 1) * 8].unsqueeze(2).to_broadcast([128, 8, R]), ALU.mult,
               ['xnT', 'muT'], [xname])
            TT('pool', X[:, :, :R], X[:, :, :R], xnT1[:, :, 1:1 + R], ALU.add, [xname, 'xnT1'], [xname])
            if c < 4:
                bks = (0, 1) if c % 2 == 0 else (4, 5)
                project(X, xname, r_win[c], 0, bks, nrow=R)
                for pi in range(2):
                    if c == 3:
                        P.op('act', lambda e, pi=pi, bks=bks: e.activation(out=sgl[:R, pi * 512:(pi + 1) * 512], in_=PB[bks[pi]][:R, :], func=AF.Silu),
                             reads=[pb[bks[pi]]], writes=['sg'])
                    else:
                        P.op('act', lambda e, pi=pi, c=c, bks=bks: e.copy(out=dsts[c][:R, pi * 512:(pi + 1) * 512], in_=PB[bks[pi]][:R, :]),
                             reads=[pb[bks[pi]]], writes=[dnames[c]])
                if c == 1:
                    emit_kk()
            else:
                ws, wn, tt_, tn, w2, w2n, dst, dn = (w1s, 'w1s', tw, 'tw', w2s, 'w2s', sgz, 'og') if c == 4 else (a1s, 'a1s', ta, 'ta', a2s, 'a2s', asg, 'ogT')
                for kc in range(8):
                    P.op('pe', lambda e, kc=kc, ws=ws, X=X: e.matmul(PB[2][0:64, :R], lhsT=ws[:, kc, :], rhs=X[:, kc, :R], start=(kc == 0), stop=(kc == 7)),
                         reads=[wn, xname], writes=[pb[2]])
                P.op('act', lambda e, tt_=tt_, c=c: e.activation(out=tt_[0:64, :R], in_=PB[2][0:64, :R], func=(AF.Tanh if c == 4 else AF.Copy)),
                     reads=[pb[2]], writes=[tn])
                for pi in range(2):
                    P.op('pe', lambda e, pi=pi, tt_=tt_, w2=w2: e.matmul(PB[pi][:R, :], lhsT=tt_[:, :R], rhs=w2[:, pi * 512:(pi + 1) * 512], start=True, stop=True),
                         reads=[tn, w2n], writes=[pb[pi]])
                    P.op('act', lambda e, pi=pi, dst=dst: e.activation(out=dst[:R, pi * 512:(pi + 1) * 512], in_=PB[pi][:R, :], func=AF.Sigmoid),
                         reads=[pb[pi]], writes=[dn])
                if c == 5:
                    emit_kh()
        for pi in range(2):
            P.op('pe', lambda e, pi=pi: e.matmul(PB[pi][:R, :], lhsT=(BDm if batched else UI)[:R, :R], rhs=sgz[:R, pi * 512:(pi + 1) * 512], start=True, stop=True),
                 reads=['cst', 'og'], writes=[pb[pi]])
            sl = slice(pi * 512, (pi + 1) * 512)
            P.op('act', lambda e, pi=pi, sl=sl: e.activation(out=gam[:R, sl], in_=PB[pi][:R, :], func=AF.Exp, scale=-CDEC), reads=[pb[pi]], writes=['xm0'])
            P.op('act', lambda e, pi=pi, sl=sl: e.activation(out=ginv[:R, sl], in_=PB[pi][:R, :], func=AF.Exp, scale=CDEC), reads=[pb[pi]], writes=['xm1'])
        P.op('act', lambda e: e.activation(out=sgz[:R, :], in_=sgz[:R, :], func=AF.Exp, scale=CDEC), reads=['og'], writes=['og'])
        TT('dve', sgz[:R, :], sgz[:R, :], gam[:R, :], ALU.mult, ['og', 'xm0'], ['og'])
        STT(sgz[:R, :], kkb[:R, :], -1.0, sgz[:R, :], ALU.mult, ALU.mult, ['S_sb', 'og'], ['og'])
        TT('dve', kkb[:R, :], kkb[:R, :], asg[:R, :], ALU.mult, ['S_sb', 'ogT'], ['S_sb'])
        TT('dve', kkb[:R, :], kkb[:R, :], ginv[:R, :], ALU.mult, ['S_sb', 'xm1'], ['S_sb'])
        TT('dve', ginv[:R, :], k_tok[:R, :], ginv[:R, :], ALU.mult, ['k_tok', 'xm1'], ['xm1'])
        if not batched:
            compute_gC(R, gam, 'xm0')
        else:
            P.dma('pool', 'sc_gam', sc['gam'][:, :], gam[:R, :], reads=['xm0'], writes=['sc_gam'])
        TT('dve', gam[:R, :], r_tok[:R, :], gam[:R, :], ALU.mult, ['q_tok', 'xm0'], ['xm0'])

    def l1_B(R, sample_seq, last_prompt):
        nlev = int(round(math.log2(R))) - 1
        if sample_seq is not None:
            load_state(sample_seq)
            compute_gC(R, xn, 'xn')
        At_, Rt_, Bt_, Kt_ = sgz, gam, kkb, ginv
        for hf in range(2):
            for (src, sname, dstf, dname, banks) in [(At_, 'og', lambda h8: ART[:, h8, 0, :R], 'ART', (0, 1)), (Rt_, 'xm0', lambda h8: ART[:, h8, 1, :R], 'ART', (2, 3)),
                                                     (Bt_, 'S_sb', lambda h8: BT[:, h8, :R], 'qT', (4, 5)), (Kt_, 'xm1', lambda h8: KT[:, h8, :R], 'KT', (6, 7))]:
                for h8 in range(8):
                    hh = hf * 8 + h8
                    bk = banks[h8 // 4]
                    P.op('pe', lambda e, src=src, hh=hh, h8=h8, bk=bk: e.transpose(PB[bk][0:64, (h8 % 4) * 128:(h8 % 4) * 128 + R],
                                                                             src[:R, hh * 64:(hh + 1) * 64], ident[:R, :R]),
                         reads=[sname, 'cst'], writes=[pb[bk]])
                for q2 in range(2):
                    bk = banks[q2]
                    if dname == 'ART':
                        which = 0 if src is At_ else 1
                        outap = ART[:, q2 * 4:(q2 + 1) * 4, which, :R]
                    elif dname == 'qT':
                        outap = BT[:, q2 * 4:(q2 + 1) * 4, :R]
                    else:
                        outap = KT[:, q2 * 4:(q2 + 1) * 4, :R]
                    P.op('act' if q2 == 0 else 'dve',
                         (lambda e, outap=outap, bk=bk: e.copy(out=outap, in_=PB[bk][0:64, :].rearrange("p (c t) -> p c t", c=4)[:, :, :R])) if q2 == 0 else
                         (lambda e, outap=outap, bk=bk: e.tensor_copy(out=outap, in_=PB[bk][0:64, :].rearrange("p (c t) -> p c t", c=4)[:, :, :R])),
                         reads=[pb[bk]], writes=[dname])
            for hs in range(2):
                heads = [hf * 8 + hs * 4 + i for i in range(4)]
                for i, hh in enumerate(heads):
                    h8 = hh % 8
                    o3 = lambda c0, i=i: PB[i][:R, c0:c0 + 256].rearrange("p (a t) -> p a t", a=2)[:, :, :R]
                    P.op('pe', lambda e, h8=h8, o3=o3: e.matmul(o3(0), lhsT=BT[:, h8, :R], rhs=ART[:, h8, :, :R], start=True, stop=True),
                         reads=['qT', 'ART'], writes=[pb[i]])
                    P.op('pe', lambda e, h8=h8, o3=o3: e.matmul(o3(256), lhsT=KT[:, h8, :R], rhs=ART[:, h8, :, :R], start=True, stop=True),
                         reads=['KT', 'ART'], writes=[pb[i]])
                    P.op('pe', lambda e, h8=h8, i=i: e.matmul(PB[4][:R, i * 128:i * 128 + R], lhsT=ART[:, h8, 0, :R], rhs=BT[:, h8, :R], start=True, stop=True),
                         reads=['qT', 'ART'], writes=[pb[4]])
                QX, PP = QP, XS
                qn = lambda b_, hf_: ('QPa', 'QPb')[b_] + str(hf_)
                pn = lambda b_, hf_: ('XSa', 'XSb')[b_] + str(hf_)
                pbank = (2, 3)
                for i in range(4):
                    hf_ = i // 2
                    TT('dve', AS[i][:R, :], PB[i][:R, :], M4[:R, :], ALU.mult, [pb[i], 'cst'], [f'AS{i}'] + (['Sld'] if i < 2 else []))
                    P.op('pool', lambda e, i=i: e.tensor_copy(out=QX[0][:R, i * 256:i * 256 + R], in_=AS[i][:R, 0:R]), reads=[f'AS{i}'], writes=[qn(0, hf_)])
                    P.op('pool', lambda e, i=i: e.tensor_copy(out=QX[0][:R, i * 256 + 128:i * 256 + 128 + R], in_=ident[:R, :R]), reads=['cst'], writes=[qn(0, hf_)])
                    TT('dve', PP[0][:R, i * 128:i * 128 + R], PB[4][:R, i * 128:i * 128 + R], LS[:R, :R], ALU.mult, [pb[4], 'cst'], [pn(0, hf_)])
                v2 = lambda ap, a: ap.rearrange("p (h a t) -> p h a t", a=2, t=128)[:, :, a, :R]
                for lv in range(1, nlev + 2):
                    sb_, db_ = (lv - 1) % 2, lv % 2
                    last = (lv == nlev + 1)
                    for hf_ in range(2):
                        for i in (2 * hf_, 2 * hf_ + 1):
                            c0 = (i % 2) * 256
                            Ppv = PP[sb_][:R, i * 128:i * 128 + R]
                            if not last:
                                P.op('pe', lambda e, i=i, hf_=hf_, c0=c0, Ppv=Ppv, sb_=sb_: e.matmul(
                                    PB[hf_][:R, c0:c0 + 256].rearrange("p (a t) -> p a t", a=2)[:, :, :R], lhsT=Ppv,
                                    rhs=QX[sb_][:R, i * 256:i * 256 + 256].rearrange("p (a t) -> p a t", a=2)[:, :, :R], start=True, stop=True),
                                    reads=[qn(sb_, hf_), pn(sb_, hf_)], writes=[pb[hf_]])
                                P.op('pe', lambda e, i=i, hf_=hf_, c0=c0, Ppv=Ppv, sb_=sb_: e.matmul(
                                    PB[pbank[hf_]][:R, (i % 2) * 128:(i % 2) * 128 + R], lhsT=QX[sb_][:R, i * 256:i * 256 + R], rhs=Ppv, start=True, stop=True),
                                    reads=[qn(sb_, hf_), pn(sb_, hf_)], writes=[pb[pbank[hf_]]])
                            else:
                                P.op('pe', lambda e, i=i, hf_=hf_, Ppv=Ppv, sb_=sb_: e.matmul(
                                    PB[pbank[hf_]][:R, (i % 2) * 128:(i % 2) * 128 + R], lhsT=Ppv, rhs=QX[sb_][:R, i * 256 + 128:i * 256 + 128 + R],
                                    start=True, stop=True),
                                    reads=[qn(sb_, hf_), pn(sb_, hf_)], writes=[pb[pbank[hf_]]])
                    for hf_ in range(2):
                        ssl = QX[sb_][:R, hf_ * 512:(hf_ + 1) * 512]
                        if not last:
                            dsl = QX[db_][:R, hf_ * 512:(hf_ + 1) * 512]
                            P.op('dve', lambda e, hf_=hf_, dsl=dsl: e.tensor_copy(out=v2(dsl, 0), in_=v2(PB[hf_][:R, :], 0)), reads=[pb[hf_]], writes=[qn(db_, hf_)])
                            TT('dve', v2(dsl, 1), v2(ssl, 1), v2(PB[hf_][:R, :], 1), ALU.add, [qn(sb_, hf_), pb[hf_]], [qn(db_, hf_)])
                            P.op('act', lambda e, hf_=hf_, db_=db_: e.copy(out=PP[db_][:R, hf_ * 256:(hf_ + 1) * 256], in_=PB[pbank[hf_]][:R, 0:256]),
                                 reads=[pb[pbank[hf_]]], writes=[pn(db_, hf_)])
                        else:
                            TT('dve', v2(ssl, 1), v2(ssl, 1), PB[pbank[hf_]][:R, 0:256].rearrange("p (h t) -> p h t", t=128)[:, :, :R], ALU.add,
                               [qn(sb_, hf_), pb[pbank[hf_]]], [qn(sb_, hf_)])
                cb_ = nlev % 2
                XFt = QX[cb_]
                xfn = lambda i: qn(cb_, i // 2)
                for i, hh in enumerate(heads):
                    h8 = hh % 8
                    P.op('pe', lambda e, i=i, hh=hh, h8=h8: e.matmul(PB[5][:R, i * 64:(i + 1) * 64], lhsT=ART[:, h8, 0, :R], rhs=Hst[:, hh, :], start=True, stop=False),
                         reads=['ART', 'Hst'], writes=[pb[5]])
                    P.op('pe', lambda e, i=i, hh=hh: e.matmul(PB[5][:R, i * 64:(i + 1) * 64], lhsT=AS[i][:R, 256:256 + R], rhs=v_tok[:R, hh * 64:(hh + 1) * 64],
                                                            start=False, stop=True),
                         reads=[f'AS{i}', 'v_tok'], writes=[pb[5]])
                P.op('act', lambda e: e.copy(out=Wsb[:R, :], in_=PB[5][:R, 0:256]), reads=[pb[5]], writes=['Wsb'])
                for i, hh in enumerate(heads):
                    P.op('pe', lambda e, i=i: e.matmul(PB[5][:R, 256 + i * 64:256 + (i + 1) * 64], lhsT=XFt[:R, i * 256 + 128:i * 256 + 128 + R], rhs=Wsb[:R, i * 64:(i + 1) * 64],
                                                     start=True, stop=True),
                         reads=[xfn(i), 'Wsb'], writes=[pb[5]])
                P.op('act', lambda e: e.copy(out=Usb[:R, :], in_=PB[5][:R, 256:512]), reads=[pb[5]], writes=['Usb'])
                for i, hh in enumerate(heads):
                    h8 = hh % 8
                    vv = v_tok[:R, hh * 64:(hh + 1) * 64]
                    uu = Usb[:R, i * 64:(i + 1) * 64]
                    yo = PB[6][:R, i * 64:(i + 1) * 64]
                    ho = PB[6][0:64, 256 + i * 64:256 + (i + 1) * 64]
                    P.op('pe', lambda e, yo=yo, h8=h8, hh=hh: e.matmul(yo, lhsT=ART[:, h8, 1, :R], rhs=Hst[:, hh, :], start=True, stop=False),
                         reads=['ART', 'Hst'], writes=[pb[6]])
                    P.op('pe', lambda e, yo=yo, i=i, uu=uu: e.matmul(yo, lhsT=AS[i][:R, 128:128 + R], rhs=uu, start=False, stop=False),
                         reads=[f'AS{i}', 'Usb'], writes=[pb[6]])
                    P.op('pe', lambda e, yo=yo, i=i, vv=vv: e.matmul(yo, lhsT=AS[i][:R, 384:384 + R], rhs=vv, start=False, stop=True),
                         reads=[f'AS{i}', 'v_tok'], writes=[pb[6]])
                    P.op('pe', lambda e, ho=ho, hh=hh, uu=uu: e.matmul(ho, lhsT=Bt_[:R, hh * 64:(hh + 1) * 64], rhs=uu, start=True, stop=False),
                         reads=['S_sb', 'Usb'], writes=[pb[6]])
                    P.op('pe', lambda e, ho=ho, hh=hh, vv=vv: e.matmul(ho, lhsT=Kt_[:R, hh * 64:(hh + 1) * 64], rhs=vv, start=False, stop=True),
                         reads=['xm1', 'v_tok'], writes=[pb[6]])
                h0 = heads[0]
                P.op('act', lambda e, h0=h0: e.copy(out=ysb[:R, h0 * 64:(h0 + 4) * 64], in_=PB[6][:R, 0:256]), reads=[pb[6]], writes=['junk', 'tok6'])
                TT('dve', Hst[:, h0:h0 + 4, :], PB[6][0:64, 256:512].rearrange("p (h v) -> p h v", h=4), Hst[:, h0:h0 + 4, :], ALU.add, [pb[6], 'Hst', 'tok6'], ['Hst'])
                TT('dve', Hst[:, h0:h0 + 4, :], Hst[:, h0:h0 + 4, :], gC[:, h0:h0 + 4].unsqueeze(2).to_broadcast([64, 4, 64]), ALU.mult, ['Hst', 'gC'], ['Hst'])
        if last_prompt or sample_seq is not None:
            store_state(sample_seq)

    def l1_C(R, hn, H, yout):
        P.op('dve', lambda e: e.tensor_reduce(out=s16[:R, 32:48], in_=v3(ysb, R), axis=AX.X, op=ALU.add), reads=['junk'], writes=['s16c'])
        P.op('dve', lambda e: e.tensor_scalar(out=s16[:R, 32:48], in0=s16[:R, 32:48], scalar1=-1.0 / 64, scalar2=None, op0=ALU.mult), reads=['s16c'], writes=['s16c'])
        TT('dve', v3(ysb, R), v3(ysb, R), s16[:R, 32:48].unsqueeze(2).to_broadcast([R, 16, 64]), ALU.add, ['junk', 's16c'], ['junk'])
        TT('dve', k_tok[:R, :], ysb[:R, :], ysb[:R, :], ALU.mult, ['junk'], ['k_tok'])
        P.op('dve', lambda e: e.tensor_reduce(out=s16[:R, 48:64], in_=v3(k_tok, R), axis=AX.X, op=ALU.add), reads=['k_tok'], writes=['s16d'])
        P.op('dve', lambda e: e.tensor_scalar(out=s16[:R, 48:64], in0=s16[:R, 48:64], scalar1=1.0 / 64, scalar2=64e-5, op0=ALU.mult, op1=ALU.add),
             reads=['s16d'], writes=['s16d'])
        P.op('act', lambda e: e.activation(out=s16[:R, 48:64], in_=s16[:R, 48:64], func=AF.Sqrt), reads=['s16d'], writes=['s16d'])
        P.op('dve', lambda e: e.reciprocal(out=s16[:R, 48:64], in_=s16[:R, 48:64]), reads=['s16d'], writes=['s16d'])
        TT('dve', v3(ysb, R), v3(ysb, R), s16[:R, 48:64].unsqueeze(2).to_broadcast([R, 16, 64]), ALU.mult, ['junk', 's16d'], ['junk'])
        TT('dve', ysb[:R, :], ysb[:R, :], vbc['ln_g'][:R, :], ALU.mult, ['junk', 'v_ln_g'], ['junk'])
        TT('dve', ysb[:R, :], ysb[:R, :], vbc['ln_b'][:R, :], ALU.add, ['junk', 'v_ln_b'], ['junk'])
        TT('dve', ysb[:R, :], ysb[:R, :], tmpA[:R, :], ALU.add, ['junk', 'PTs'], ['junk'])
        TT('dve', ysb[:R, :], ysb[:R, :], sgl[:R, :], ALU.mult, ['junk', 'sg'], ['junk'])
        transpose_to(ysb, 'junk', xnT, 'xnT', (2, 3), nrow=R)
        project(xnT, 'xnT', r_wout, 0, (0, 1), nrow=R)
        for pi in range(2):
            TT('dve', H[:R, pi * 512:(pi + 1) * 512], PB[pi][:R, :], H[:R, pi * 512:(pi + 1) * 512], ALU.add, [pb[pi], hn], [hn])
        if yout is not None:
            rms_rstd(H, hn, R)
            STT(xn[:R, :], H[:R, :], rstd[:R, :], vbc['gf'][:R, :], ALU.mult, ALU.mult, [hn, 'rstd', 'v_gf'], ['xn'])
            P.dma('pool', 'o_y', yout, xn[:R, :], reads=['xn'])

    def store_state(sample_seq):
        for hh in range(16):
            bk = 4 + hh // 8
            P.op('pe', lambda e, hh=hh, bk=bk: e.transpose(PB[bk][0:64, (hh % 8) * 64:(hh % 8) * 64 + 64], Hst[:, hh, :], ident[:64, :64]),
                 reads=['Hst', 'cst'], writes=[pb[bk]])
        for half in range(2):
            P.op('act', lambda e, half=half: e.copy(out=Sld[:, half * 8:(half + 1) * 8, :], in_=PB[4 + half][0:64, :].rearrange("p (h k) -> p h k", h=8)),
                 reads=[pb[4 + half]], writes=['Sld', 'AS0', 'AS1'])
        dst = o_wkv if sample_seq is None else o_wkvs[sample_seq]
        P.dma('pool', 'o_S', dst.rearrange("h v k -> v h k"), Sld, reads=['Sld', 'AS0', 'AS1'])

    nt = NT if stage != 1 else 3
    for t in range(nt):
        hn = 'hA' if t % 2 == 0 else 'hB'
        H = hb[t % 2]
        cm_ap = cst[:, C_CM0:C_CM0 + 256] if t == 0 else (cst[:, C_CM1:C_CM1 + 256] if t == 1 else None)
        l0_tile(128, xp[t * 128:(t + 1) * 128, :], hn, H, cm_ap, last_prompt=(t == nt - 1))
        if stage == 0:
            if t >= 1:
                P.dma('pool', 'o_y' + hn, yp[(t - 1) * 128:t * 128, :], H[:], reads=[hn])
        else:
            l1_tile(128, hn, H, last_prompt=(t == nt - 1), yout=(yp[(t - 1) * 128:t * 128, :] if t >= 1 else None))
    if stage >= 2:
        H, hn = hb[0], 'hA'
        l0_tile(NTOK, d_xs[:, :], hn, H, None, parts='A')
        for nm, src, sn in [('q', q_tok, 'q_tok'), ('kv', kv_tok, 'kv_tok'), ('sg', sg, 'sg')]:
            P.dma('pool', 'sc_' + nm, sc[nm][:, :], src[:NTOK, :], reads=[sn], writes=['sc_' + nm])
        for s_ in range(NSEQ):
            r0 = 4 * s_
            for nm, dst, dn in [('q', q_tok, 'q_tok'), ('kv', kv_tok, 'kv_tok'), ('sg', sg, 'sg')]:
                P.dma('sp', 'ld_' + nm, dst[0:4, :], sc[nm][r0:r0 + 4, :], reads=['sc_' + nm], writes=[dn])
            l0_tile(4, None, hn, H, None, sample_seq=s_, parts='B')
            P.dma('pool', 'sc_og', sc['og'][r0:r0 + 4, :], og[0:4, :], reads=['og'], writes=['sc_og'])
        P.dma('sp', 'ld_og', og[:NTOK, :], sc['og'][:, :], reads=['sc_og'], writes=['og'])
        l0_tile(NTOK, None, hn, H, None, parts='C')
        l1_tile(NTOK, hn, H, parts='A', batched=True)
        l1src = [('At', sgz, 'og'), ('Rt', gam, 'xm0'), ('Bt', kkb, 'S_sb'), ('Kt', ginv, 'xm1'), ('v', v_tok, 'v_tok')]
        for nm, src, sn in l1src:
            P.dma('pool', 'sc_' + nm, sc[nm][:, :], src[:NTOK, :], reads=[sn], writes=['sc_' + nm])
        for s_ in range(NSEQ):
            r0 = 4 * s_
            for nm, dst, dn in l1src + [('gam', xn, 'xn')]:
                P.dma('sp', 'ld_' + nm, dst[0:4, :], sc[nm][r0:r0 + 4, :], reads=['sc_' + nm], writes=[dn])
            l1_tile(4, hn, H, sample_seq=s_, parts='B')
            P.dma('pool', 'sc_y', sc['y'][r0:r0 + 4, :], ysb[0:4, :], reads=['junk'], writes=['sc_y'])
        P.dma('sp', 'ld_y', ysb[:NTOK, :], sc['y'][:, :], reads=['sc_y'], writes=['junk'])
        l1_tile(NTOK, hn, H, parts='C', yout=ys[:, :])
    P.finish('pool')
    info = dict(cnt=dict(P.cnt), nwaits=P.nwaits, sbuf_left=nc.sbuf_bytes_remaining)
    return nc, st, info


def kernel(x_prompt, x_sample, cache_win_k, cache_win_v, state_wkv, state_shift,
           meta_tokens, rel_bias_table, norm_gain, final_gain,
           attn_w_in, attn_sinks, attn_w_out,
           rwkv_mu, rwkv_w_in, rwkv_w0, rwkv_w1, rwkv_w2, rwkv_a0, rwkv_a1, rwkv_a2,
           rwkv_k_k, rwkv_k_a, rwkv_r_k, rwkv_ln_gamma, rwkv_ln_beta, rwkv_w_out, _stage=9):
    f = lambda a: np.ascontiguousarray(np.asarray(a, dtype=np.float32))
    nc, st, info = build(_stage)
    cst, E = host_consts()
    vecs = np.stack([f(norm_gain)[0], f(norm_gain)[1], f(final_gain), f(rwkv_k_k)[0], f(rwkv_k_a)[0],
                     f(rwkv_r_k)[0].reshape(-1), f(rwkv_ln_gamma)[0], f(rwkv_ln_beta)[0]])
    tab33 = np.concatenate([f(rel_bias_table), np.full((1, 16), NEG, np.float32)], 0)
    shared = dict(a_win=f(attn_w_in)[0], a_wout=f(attn_w_out)[0], r_win=f(rwkv_w_in)[0], r_wout=f(rwkv_w_out)[0],
                  w1=f(rwkv_w1)[0], a1=f(rwkv_a1)[0],
                  w2w0=np.concatenate([f(rwkv_w2)[0], f(rwkv_w0)], 0), a2a0=np.concatenate([f(rwkv_a2)[0], f(rwkv_a0)], 0),
                  vecs=f(vecs), mu=f(rwkv_mu)[0], tab33=f(tab33), sinks=f(attn_sinks), cst=cst, E=E)
    pad = np.zeros((112, 1024), np.float32)
    in_maps = []
    for c in range(8):
        m = dict(shared)
        m["xp"] = np.ascontiguousarray(np.concatenate([pad, f(meta_tokens), f(x_prompt)[c % 4]], 0)[:NT * 128])
        sq = slice(NSEQ * c, NSEQ * (c + 1))
        m["xs"] = np.ascontiguousarray(f(x_sample)[sq].reshape(NTOK, 1024))
        m["ck"] = np.ascontiguousarray(f(cache_win_k)[0, sq].reshape(NSEQ, 128, 256))
        m["cv"] = np.ascontiguousarray(f(cache_win_v)[0, sq].reshape(NSEQ, 128, 256))
        m["swkv"] = np.ascontiguousarray(f(state_wkv)[0, sq])
        m["ssh"] = np.ascontiguousarray(f(state_shift)[0, sq])
        in_maps.append(m)
    res = run_bass_kernel_spmd(nc, in_maps, core_ids=list(range(8)))
    st.close()
    R = res.results
    y_prompt = np.stack([R[b]["yp"] for b in range(4)])
    wk = np.stack([R[b]["o_wk"].reshape(128, 4, 64) for b in range(4)])[None]
    wv = np.stack([R[b]["o_wv"].reshape(128, 4, 64) for b in range(4)])[None]
    wkv_p = np.stack([R[b]["o_wkv"] for b in range(4)])[None]
    sh_p = np.stack([R[b]["o_shp"][0] for b in range(4)])[None]
    y_sample = np.concatenate([R[c]["ys"].reshape(NSEQ, 4, 1024) for c in range(8)], 0)
    wks = np.concatenate([R[c]["o_wks"].reshape(NSEQ, 128, 4, 64) for c in range(8)], 0)[None]
    wvs = np.concatenate([R[c]["o_wvs"].reshape(NSEQ, 128, 4, 64) for c in range(8)], 0)[None]
    wkvs = np.concatenate([R[c]["o_wkvs"] for c in range(8)], 0)[None]
    shs = np.concatenate([R[c]["o_shs"] for c in range(8)], 0)[None]
    return (y_prompt, y_sample, wk, wv, wkv_p, sh_p, wks, wvs, wkvs, shs)
```

```python
import math
from contextlib import ExitStack
import numpy as np
import concourse.bass as bass
import concourse.mybir as mybir
from concourse.bass_utils import run_bass_kernel_spmd

F32 = mybir.dt.float32
AF = mybir.ActivationFunctionType
ALU = mybir.AluOpType
AX = mybir.AxisListType

NT = 33
NEG = -30000.0
CDEC = math.exp(-0.5)
NSEQ = 16
NTOK = 64


class Prog:
    EPOCH = 12000

    def __init__(self, nc, st):
        self.nc, self.st = nc, st
        self.eng = dict(pe=nc.tensor, act=nc.scalar, dve=nc.vector, pool=nc.gpsimd, sp=nc.sync)
        self.cnt = {e: 0 for e in self.eng}
        self.sems = {e: [] for e in self.eng}
        self.seen = {e: {} for e in self.eng}
        self.lastw, self.readers, self.streams = {}, {}, {}
        self.nwaits = 0

    def _sem(self, e, t):
        ep = (t - 1) // self.EPOCH
        while len(self.sems[e]) <= ep:
            self.sems[e].append(self.st.enter_context(self.nc.semaphore(f"s_{e}_{len(self.sems[e])}")))
        return self.sems[e][ep], t - ep * self.EPOCH

    def _wait(self, e, src, t):
        if t <= self.seen[e].get(src, 0):
            return
        if src in self.eng:
            sem, v = self._sem(src, t)
            self.eng[e].wait_ge(sem, v)
        else:
            self.eng[e].wait_ge(self.streams[src][0], 16 * t)
        self.nwaits += 1
        self.seen[e][src] = t

    def _deps(self, e, reads, writes, is_dma):
        need = {}

        def add(s, t, kind):
            if s == e and not is_dma:
                if e == 'pe' or kind == 'war':
                    return
            if need.get(s, 0) < t:
                need[s] = t
        for b in reads:
            w = self.lastw.get(b)
            if w:
                add(w[0], w[1], 'raw')
        for b in writes:
            w = self.lastw.get(b)
            if w:
                add(w[0], w[1], 'waw')
            for s, t in self.readers.get(b, {}).items():
                add(s, t, 'war')
        for s, t in need.items():
            self._wait(e, s, t)

    def _commit(self, tk, reads, writes):
        for b in reads:
            self.readers.setdefault(b, {})[tk[0]] = tk[1]
        for b in writes:
            self.lastw[b] = tk
            self.readers[b] = {}

    def op(self, e, fn, reads=(), writes=()):
        self._deps(e, reads, writes, False)
        inst = fn(self.eng[e])
        self.cnt[e] += 1
        t = self.cnt[e]
        inst.then_inc(self._sem(e, t)[0], 1)
        self._commit((e, t), reads, writes)

    def dma(self, q, stream, out, in_, reads=(), writes=()):
        self._deps(q, reads, writes, True)
        if stream not in self.streams:
            self.streams[stream] = [self.st.enter_context(self.nc.semaphore(f"d_{stream}")), 0]
        s = self.streams[stream]
        inst = self.eng[q].dma_start(out=out, in_=in_)
        s[1] += 1
        inst.then_inc(s[0], 16)
        self._commit((stream, s[1]), reads, writes)

    def finish(self, e='pool'):
        for name, (sem, n) in self.streams.items():
            self._wait(e, name, n)


def t5_bucket(rel):
    n = max(rel, 0)
    if n < 16:
        return n
    nf = np.float32(max(n, 16))
    large = 16 + int(np.float32(np.log(nf / np.float32(16)) / np.float32(math.log(128 / 16))) * np.float32(16))
    return min(large, 31)


def host_consts():
    ident = np.eye(128, dtype=np.float32)
    idx = np.arange(128)
    Us = (idx[:, None] < idx[None, :]).astype(np.float32)
    Ui = (idx[:, None] <= idx[None, :]).astype(np.float32)
    Ls = Us.T.copy()
    e127 = np.zeros((128, 1), np.float32)
    e127[127] = 1
    cm = np.zeros((2, 128, 256), np.float32)
    cm[0][:, :240] = NEG
    cm[1][:, :112] = NEG
    BD = np.zeros((128, 64), np.float32)
    for j in range(64):
        for i in range(64):
            if j // 4 == i // 4 and j <= i:
                BD[j, i] = 1
    cst = np.concatenate([ident, Us, Ui, Us, Ui, Ls, e127, cm[0], cm[1], BD], axis=1)
    E = np.zeros((33, 383), np.float32)
    for i in range(383):
        rel = i - 127
        if 0 <= rel < 128:
            E[t5_bucket(rel), i] = 1
        else:
            E[32, i] = 1
    return np.ascontiguousarray(cst), E


C_ID, C_M4, C_LS, C_E127, C_CM0, C_CM1 = 0, 128, 640, 768, 769, 1025
C_BD = 1281
CSTW = 1345


def build(stage=9):
    nc = bass.Bass("TRN2", target_bir_lowering=False)
    st = ExitStack()
    P = Prog(nc, st)

    def din(n, s):
        return nc.dram_tensor(n, list(s), F32, kind="ExternalInput").ap()

    def dout(n, s):
        return nc.dram_tensor(n, list(s), F32, kind="ExternalOutput").ap()

    def T(n, s):
        return st.enter_context(nc.sbuf_tensor("sb_" + n, list(s), F32))

    xp = din("xp", [NT * 128, 1024])
    a_win = din("a_win", [1024, 2560])
    a_wout = din("a_wout", [1024, 1024])
    r_win = din("r_win", [4, 1024, 1024])
    r_wout = din("r_wout", [1024, 1024])
    d_w1 = din("w1", [1024, 64])
    d_a1 = din("a1", [1024, 64])
    d_w2 = din("w2w0", [65, 1024])
    d_a2 = din("a2a0", [65, 1024])
    d_vec = din("vecs", [8, 1024])
    d_mu = din("mu", [6, 1024])
    d_tab = din("tab33", [33, 16])
    d_sink = din("sinks", [1, 16])
    d_cst = din("cst", [128, CSTW])
    d_E = din("E", [33, 383])
    yp = dout("yp", [4096, 1024])
    o_wk = dout("o_wk", [128, 256])
    o_wv = dout("o_wv", [128, 256])
    o_wkv = dout("o_wkv", [16, 64, 64])
    o_shp = dout("o_shp", [1, 1024])

    cst = T("cst", [128, CSTW])
    ident = cst[:, C_ID:C_ID + 128]
    bias = T("bias", [128, 16, 256])
    sinkbc = T("sinkbc", [128, 16])
    vbc = {n: T("v_" + n, [128, 1024]) for n in ["g1", "gf", "k_k", "k_a", "r_k", "ln_g", "ln_b"]}
    VIDX = dict(g0=0, g1=1, gf=2, k_k=3, k_a=4, r_k=5, ln_g=6, ln_b=7)
    g0T = T("g0T", [128, 8])
    wp = [T(f"wp{i}", [128, 1024]) for i in range(4)]
    hb = [T("hA", [128, 1024]), T("hB", [128, 1024])]
    xn = T("xn", [128, 1024])
    xnT = T("xnT", [128, 8, 128])
    q_tok = T("q_tok", [128, 1024])
    kv_tok = T("kv_tok", [128, 512])
    sg = T("sg", [128, 1024])
    qT = T("qT", [64, 16, 128])
    kT = T("kT", [64, 4, 256])
    vprev = T("vprev", [128, 256])
    S_sb = T("S_sb", [128, 4, 256])
    S_sb1 = T("S_sb1", [128, 4, 256])
    PTs = T("PTs", [128, 8, 128])
    og = T("og", [128, 1024])
    ogT = T("ogT", [128, 8, 128])
    small = T("small", [128, 64])
    junk = T("junk", [128, 1024])
    PB = [st.enter_context(nc.psum_tensor(f"PB{i}", [128, 512], F32)) for i in range(8)]
    pb = [f"PB{i}" for i in range(8)]

    ss, ms, rstd, rstd8 = small[:, 0:1], small[:, 1:2], small[:, 2:3], small[:, 3:4]
    mx, negm, rs, es, den = small[:, 4:8], small[:, 8:12], small[:, 12:16], small[:, 16:20], small[:, 20:24]

    P.dma('sp', 'c_cst', cst[:], d_cst[:, :], writes=['cst'])
    for n, tl in vbc.items():
        src = bass.AP(d_vec.tensor, VIDX[n] * 1024, [[0, 128], [1, 1024]])
        P.dma('sp', 'c_' + n, tl[:], src, writes=['v_' + n])
    P.dma('sp', 'c_sink', sinkbc[:], bass.AP(d_sink.tensor, 0, [[0, 128], [1, 16]]), writes=['sinkbc'])
    g0r = T("g0r", [8, 128])
    P.dma('sp', 'c_g0r', g0r[:], bass.AP(d_vec.tensor, 0, [[128, 8], [1, 128]]), writes=['g0r'])
    P.op('pe', lambda e: e.transpose(PB[0][:, 0:8], g0r[:, :], cst[0:8, C_ID:C_ID + 8]), reads=['g0r', 'cst'], writes=[pb[0]])
    P.op('act', lambda e: e.copy(out=g0T[:], in_=PB[0][:, 0:8]), reads=[pb[0]], writes=['g0T'])
    Esb = T("Esb", [33, 383])
    tab = T("tab", [33, 16])
    P.dma('sp', 'c_E', Esb[:], d_E[:, :], writes=['Esb'])
    P.dma('sp', 'c_tab', tab[:], d_tab[:, :], writes=['tab'])
    for r8 in range(8):
        bk = r8 % 2
        for sl in range(32):
            s = r8 * 32 + sl
            P.op('pe', lambda e, s=s, sl=sl, bk=bk: e.matmul(PB[bk][:, sl * 16:(sl + 1) * 16], lhsT=Esb[:, 255 - s:255 - s + 128],
                                                           rhs=tab[:, :], start=True, stop=True),
                 reads=['Esb', 'tab'], writes=[pb[bk]])
        P.op('act', lambda e, r8=r8, bk=bk: e.copy(out=bias[:, :, r8 * 32:(r8 + 1) * 32].rearrange("q h s -> q s h"),
                                                  in_=PB[bk][:, :].rearrange("q (s h) -> q s h", h=16)),
             reads=[pb[bk]], writes=['bias'])
    P.op('pool', lambda e: e.memset(kT[:], 0.0), writes=['kT'])
    P.op('pool', lambda e: e.memset(vprev[:], 0.0), writes=['vprev'])

    slot_i = [0]

    def project(xT, xTname, wsrc2d, c0, banks, nrow=128, ncol=1024):
        nb = (ncol + 511) // 512
        for kc in range(8):
            i = slot_i[0] % 4
            slot_i[0] += 1
            P.dma('sp', f'wp{i}', wp[i][:, 0:ncol], wsrc2d[kc * 128:(kc + 1) * 128, c0:c0 + ncol], writes=[f'wp{i}'])
            for b in range(nb):
                w = min(512, ncol - b * 512)
                P.op('pe', lambda e, i=i, b=b, w=w, kc=kc: e.matmul(PB[banks[b]][:nrow, :w], lhsT=xT[:, kc, :nrow], rhs=wp[i][:, b * 512:b * 512 + w],
                                                                  start=(kc == 0), stop=(kc == 7)),
                     reads=[xTname, f'wp{i}'], writes=[pb[banks[b]]])

    def rms_rstd(src, srcname, nrow=128):
        P.op('act', lambda e: e.activation(out=junk[:nrow, :], in_=src[:nrow, :], func=AF.Square, accum_out=ss[:nrow, :]),
             reads=[srcname], writes=['junk', 'ss'])
        P.op('dve', lambda e: e.tensor_scalar(out=ms[:nrow, :], in0=ss[:nrow, :], scalar1=1.0 / 1024, scalar2=1e-6, op0=ALU.mult, op1=ALU.add),
             reads=['ss'], writes=['ms'])
        P.op('act', lambda e: e.activation(out=ms[:nrow, :], in_=ms[:nrow, :], func=AF.Sqrt), reads=['ms'], writes=['ms'])
        P.op('dve', lambda e: e.reciprocal(out=rstd[:nrow, :], in_=ms[:nrow, :]), reads=['ms'], writes=['rstd'])

    def transpose_to(src, srcname, dst, dstname, banks, nrow=128, scale_cols=None):
        for c in range(8):
            bk = banks[c // 4]
            P.op('pe', lambda e, c=c, bk=bk: e.transpose(PB[bk][:, (c % 4) * 128:(c % 4) * 128 + nrow], src[:nrow, c * 128:(c + 1) * 128],
                                                        ident[:nrow, :nrow]),
                 reads=[srcname, 'cst'], writes=[pb[bk]])
        for half in range(2):
            bk = banks[half]
            if scale_cols is None:
                P.op('act', lambda e, half=half, bk=bk: e.copy(out=dst[:, half * 4:(half + 1) * 4, :nrow],
                                                            in_=PB[bk][:, :].rearrange("p (c t) -> p c t", c=4)[:, :, :nrow]),
                     reads=[pb[bk]], writes=[dstname])
            else:
                P.op('dve', lambda e, half=half, bk=bk: e.tensor_tensor(
                    out=dst[:, half * 4:(half + 1) * 4, :nrow], in0=PB[bk][:, :].rearrange("p (c t) -> p c t", c=4)[:, :, :nrow],
                    in1=scale_cols[:, half * 4:(half + 1) * 4].unsqueeze(2).to_broadcast([128, 4, nrow]), op=ALU.mult),
                    reads=[pb[bk], 'g0T'], writes=[dstname])

    def attention(nq, kT_ap_fn, nk, vblocks, cm_ap, O_banks):
        nb = len(vblocks)

        def scores(g):
            for j in range(4):
                hh = 4 * g + j
                bk = j // 2
                P.op('pe', lambda e, hh=hh, j=j, bk=bk, g=g: e.matmul(PB[bk][:nq, (j % 2) * 256:(j % 2) * 256 + nk], lhsT=qT[:, hh, :nq],
                                                                   rhs=kT_ap_fn(g), start=True, stop=True),
                     reads=['qT', 'kT'], writes=[pb[bk]])

        def softmax(g, S, sn):
            for bk in range(2):
                P.op('dve', lambda e, bk=bk, g=g: e.tensor_tensor(
                    out=S[:nq, 2 * bk:2 * bk + 2, :nk], in0=PB[bk][:nq, :].rearrange("q (a s) -> q a s", a=2)[:, :, :nk],
                    in1=bias[:nq, 4 * g + 2 * bk:4 * g + 2 * bk + 2, :nk], op=ALU.add),
                    reads=[pb[bk], 'bias'], writes=[sn])
            if cm_ap is not None:
                P.op('dve', lambda e: e.tensor_tensor(out=S[:nq, :, :nk], in0=S[:nq, :, :nk],
                                                      in1=cm_ap[:nq, :nk].unsqueeze(1).to_broadcast([nq, 4, nk]), op=ALU.add),
                     reads=[sn, 'cst'], writes=[sn])
            P.op('dve', lambda e: e.tensor_reduce(out=mx[:nq, :], in_=S[:nq, :, :nk], axis=AX.X, op=ALU.max), reads=[sn], writes=['mx'])
            P.op('dve', lambda e, g=g: e.tensor_tensor(out=mx[:nq, :], in0=mx[:nq, :], in1=sinkbc[:nq, 4 * g:4 * g + 4], op=ALU.max),
                 reads=['mx', 'sinkbc'], writes=['mx'])
            P.op('dve', lambda e: e.tensor_scalar(out=negm[:nq, :], in0=mx[:nq, :], scalar1=-1.0, scalar2=None, op0=ALU.mult),
                 reads=['mx'], writes=['negm'])
            P.op('dve', lambda e, g=g: e.tensor_tensor(out=es[:nq, :], in0=sinkbc[:nq, 4 * g:4 * g + 4], in1=negm[:nq, :], op=ALU.add),
                 reads=['negm', 'sinkbc'], writes=['es'])
            for j in range(4):
                P.op('act', lambda e, j=j: e.activation(out=S[:nq, j, :nk], in_=S[:nq, j, :nk], func=AF.Exp, bias=negm[:nq, j:j + 1],
                                                        accum_out=rs[:nq, j:j + 1]),
                     reads=[sn, 'negm'], writes=[sn, 'rs'])
            P.op('act', lambda e: e.activation(out=es[:nq, :], in_=es[:nq, :], func=AF.Exp), reads=['es'], writes=['es'])
            P.op('dve', lambda e: e.tensor_tensor(out=den[:nq, :], in0=rs[:nq, :], in1=es[:nq, :], op=ALU.add), reads=['rs', 'es'], writes=['den'])
            P.op('dve', lambda e: e.reciprocal(out=den[:nq, :], in_=den[:nq, :]), reads=['den'], writes=['den'])
            P.op('dve', lambda e: e.tensor_tensor(out=S[:nq, :, :nk], in0=S[:nq, :, :nk],
                                                  in1=den[:nq, :].unsqueeze(2).to_broadcast([nq, 4, nk]), op=ALU.mult),
                 reads=[sn, 'den'], writes=[sn])

        def pv(g, S, sn):
            for j in range(4):
                for b in range(nb):
                    rows = vblocks[b][1]
                    idx = j * 2 + b
                    bk = 2 + idx // 4
                    P.op('pe', lambda e, j=j, b=b, rows=rows, idx=idx, bk=bk: e.transpose(
                        PB[bk][:rows, (idx % 4) * 128:(idx % 4) * 128 + nq], S[:nq, j, b * 128:b * 128 + rows], ident[:nq, :nq]),
                        reads=[sn, 'cst'], writes=[pb[bk]])
            for half in range(2):
                bk = 2 + half
                P.op('act', lambda e, half=half, bk=bk: e.copy(out=PTs[:, half * 4:(half + 1) * 4, :nq],
                                                            in_=PB[bk][:, :].rearrange("p (c t) -> p c t", c=4)[:, :, :nq]),
                     reads=[pb[bk]], writes=['PTs'])
            for j in range(4):
                hh = 4 * g + j
                ob = O_banks[hh // 8]
                for b in range(nb):
                    vfn, rows, vname = vblocks[b]
                    P.op('pe', lambda e, j=j, b=b, hh=hh, ob=ob, vfn=vfn, rows=rows, g=g: e.matmul(
                        PB[ob][:nq, (hh % 8) * 64:(hh % 8) * 64 + 64], lhsT=PTs[:rows, j * 2 + b, :nq], rhs=vfn(g),
                        start=(b == 0), stop=(b == nb - 1)),
                        reads=['PTs', vname], writes=[pb[ob]])

        Sb = [(S_sb, 'S_sb'), (S_sb1, 'S_sb1')]
        scores(0)
        for g in range(4):
            S, sn = Sb[g % 2]
            softmax(g, S, sn)
            if g + 1 < 4:
                scores(g + 1)
            pv(g, S, sn)

    d_xs = din("xs", [NTOK, 1024])
    d_ck = din("ck", [NSEQ, 128, 256])
    d_cv = din("cv", [NSEQ, 128, 256])
    d_swkv = din("swkv", [NSEQ, 16, 64, 64])
    d_ssh = din("ssh", [NSEQ, 1024])
    ys = dout("ys", [NTOK, 1024])
    o_wks = dout("o_wks", [NSEQ, 128, 256])
    o_wvs = dout("o_wvs", [NSEQ, 128, 256])
    o_wkvs = dout("o_wkvs", [NSEQ, 16, 64, 64])
    o_shs = dout("o_shs", [NSEQ, 1024])

    xnT1 = T("xnT1", [128, 8, 129])
    xm = [T("xm0", [128, 8, 128]), T("xm1", [128, 8, 128])]
    k_tok = T("k_tok", [128, 1024])
    v_tok = T("v_tok", [128, 1024])
    ART = T("ART", [64, 8, 2, 128])
    KT = T("KT", [64, 8, 128])
    Hst = T("Hst", [64, 16, 64])
    gC = T("gC", [64, 16])
    ASall = T("ASall", [128, 4, 512])
    AS = [ASall[:, i, :] for i in range(4)]
    QP = [T("QPa", [128, 1024]), T("QPb", [128, 1024])]
    XS = [T("XSa", [128, 512]), T("XSb", [128, 512])]
    Wsb = T("Wsb", [128, 256])
    Usb = T("Usb", [128, 256])
    w1s = T("w1s", [128, 8, 64])
    a1s = T("a1s", [128, 8, 64])
    w2s = T("w2s", [65, 1024])
    a2s = T("a2s", [65, 1024])
    tw = T("tw", [65, 128])
    ta = T("ta", [65, 128])
    muT = T("muT", [128, 48])
    mur = T("mur", [48, 128])
    s16 = T("s16", [128, 64])
    ck_tok = T("ck_tok", [128, 256])
    shr = T("shr", [8, 128])
    Sld = ASall[0:64, 0:2, :].rearrange("p a b -> p (a b)").rearrange("p (h k) -> p h k", h=16)
    r_tok, sgl, sgz, asg = q_tok, sg, og, ogT[:, :, :].rearrange("p c t -> p (c t)")
    kkb = S_sb[:, :, :].rearrange("p a s -> p (a s)")
    tmpA = PTs[:, :, :].rearrange("p c t -> p (c t)")
    ysb = junk
    dxT = xnT
    gam = xm[0][:, :, :].rearrange("p c t -> p (c t)")
    ginv = xm[1][:, :, :].rearrange("p c t -> p (c t)")
    BT = qT
    P.dma('sp', 'c_w1', w1s[:], d_w1.rearrange("(c p) n -> p c n", p=128), writes=['w1s'])
    P.dma('sp', 'c_a1', a1s[:], d_a1.rearrange("(c p) n -> p c n", p=128), writes=['a1s'])
    P.dma('sp', 'c_w2', w2s[:], d_w2[:, :], writes=['w2s'])
    P.dma('sp', 'c_a2', a2s[:], d_a2[:, :], writes=['a2s'])
    P.dma('sp', 'c_mur', mur[:], bass.AP(d_mu.tensor, 0, [[128, 48], [1, 128]]), writes=['mur'])
    P.op('pe', lambda e: e.transpose(PB[1][:, 0:48], mur[:, :], cst[0:48, C_ID:C_ID + 48]), reads=['mur', 'cst'], writes=[pb[1]])
    P.op('act', lambda e: e.copy(out=muT[:], in_=PB[1][:, 0:48]), reads=[pb[1]], writes=['muT'])
    P.op('pool', lambda e: e.memset(tw[:], 1.0), writes=['tw'])
    P.op('pool', lambda e: e.memset(ta[:], 1.0), writes=['ta'])
    P.op('pool', lambda e: e.memset(xnT1[:], 0.0), writes=['xnT1'])
    P.op('pool', lambda e: e.memset(Hst[:], 0.0), writes=['Hst'])
    v3 = lambda ap, R: ap[:R, :].rearrange("p (h d) -> p h d", h=16)
    bc16 = lambda ap, R: ap[:R, 0:16].unsqueeze(2).to_broadcast([R, 16, 64])
    M4 = cst[:, C_M4:C_M4 + 512]
    LS = cst[:, C_LS:C_LS + 128]
    UI = cst[:, C_M4 + 128:C_M4 + 256]
    BDm = cst[:, C_BD:C_BD + 64]
    xpT = T("xpT", [128, 8, 64])
    sc = {n: nc.dram_tensor("sc_" + n, [NTOK, w], F32).ap() for n, w in
          [('q', 1024), ('kv', 512), ('sg', 1024), ('og', 1024), ('At', 1024), ('Rt', 1024), ('Bt', 1024), ('Kt', 1024), ('v', 1024),
           ('gam', 1024), ('y', 1024)]}

    def l0_tile(R, xsrc, hn, H, cm_ap, sample_seq=None, last_prompt=False, parts='ABC'):
        if 'A' in parts:
            l0_A(R, xsrc, hn, H)
        if 'B' in parts:
            l0_B(R, cm_ap, sample_seq, last_prompt)
        if 'C' in parts:
            l0_C(R, hn, H)

    def l0_A(R, xsrc, hn, H):
        P.dma('sp', hn, H[:R, :], xsrc, writes=[hn])
        transpose_to(H, hn, xnT, 'xnT', (2, 3), nrow=R, scale_cols=g0T)
        rms_rstd(H, hn, R)
        P.op('dve', lambda e: e.tensor_scalar(out=rstd8[:R, :], in0=rstd[:R, :], scalar1=0.125, scalar2=None, op0=ALU.mult),
             reads=['rstd'], writes=['rstd8'])
        project(xnT, 'xnT', a_win, 0, (0, 1), nrow=R)
        for pi in range(2):
            P.op('act', lambda e, pi=pi: e.activation(out=q_tok[:R, pi * 512:(pi + 1) * 512], in_=PB[pi][:R, :], func=AF.Copy, scale=rstd8[:R, :]),
                 reads=[pb[pi], 'rstd8'], writes=['q_tok'])
        project(xnT, 'xnT', a_win, 1024, (2,), nrow=R, ncol=512)
        P.op('dve', lambda e: e.tensor_scalar(out=kv_tok[:R, :], in0=PB[2][:R, :], scalar1=rstd[:R, :], scalar2=None, op0=ALU.mult),
             reads=[pb[2], 'rstd'], writes=['kv_tok'])
        project(xnT, 'xnT', a_win, 1536, (3, 4), nrow=R)
        for pi in range(2):
            P.op('act', lambda e, pi=pi: e.activation(out=sg[:R, pi * 512:(pi + 1) * 512], in_=PB[3 + pi][:R, :], func=AF.Silu, scale=rstd[:R, :]),
                 reads=[pb[3 + pi], 'rstd'], writes=['sg'])

    def l0_B(R, cm_ap, sample_seq, last_prompt):
        for hh in range(16):
            bk = 4 + hh // 4
            P.op('pe', lambda e, hh=hh, bk=bk: e.transpose(PB[bk][0:64, (hh % 4) * 128:(hh % 4) * 128 + R], q_tok[:R, hh * 64:(hh + 1) * 64],
                                                        ident[:R, :R]),
                 reads=['q_tok', 'cst'], writes=[pb[bk]])
        for q4 in range(4):
            P.op('act', lambda e, q4=q4: e.copy(out=qT[:, q4 * 4:(q4 + 1) * 4, :R],
                                               in_=PB[4 + q4][0:64, :].rearrange("p (c t) -> p c t", c=4)[:, :, :R]),
                 reads=[pb[4 + q4]], writes=['qT'])
        if sample_seq is not None:
            s_ = sample_seq
            P.dma('sp', 'ck_tok', ck_tok[:], d_ck[s_, :, :], writes=['ck_tok'])
            P.dma('sp', 'vprev', vprev[:], d_cv[s_, :, :], writes=['vprev'])
            for kv in range(4):
                P.op('pe', lambda e, kv=kv: e.transpose(PB[3][0:64, kv * 128:(kv + 1) * 128], ck_tok[:, kv * 64:(kv + 1) * 64], ident),
                     reads=['ck_tok', 'cst'], writes=[pb[3]])
            P.op('act', lambda e: e.copy(out=kT[:, :, 0:128], in_=PB[3][0:64, :].rearrange("p (c t) -> p c t", c=4)), reads=[pb[3]], writes=['kT'])
            P.dma('pool', 'o_wks', o_wks[s_, 0:124, :], ck_tok[4:128, :], reads=['ck_tok'])
            P.dma('pool', 'o_wvs', o_wvs[s_, 0:124, :], vprev[4:128, :], reads=['vprev'])
            P.dma('pool', 'o_wks2', o_wks[s_, 124:128, :], kv_tok[0:4, 0:256], reads=['kv_tok'])
            P.dma('pool', 'o_wvs2', o_wvs[s_, 124:128, :], kv_tok[0:4, 256:512], reads=['kv_tok'])
        for kv in range(4):
            P.op('pe', lambda e, kv=kv: e.transpose(PB[2][0:64, kv * 128:kv * 128 + R], kv_tok[:R, kv * 64:(kv + 1) * 64], ident[:R, :R]),
                 reads=['kv_tok', 'cst'], writes=[pb[2]])
        P.op('act', lambda e: e.copy(out=kT[:, :, 128:128 + R], in_=PB[2][0:64, :].rearrange("p (c t) -> p c t", c=4)[:, :, :R]),
             reads=[pb[2]], writes=['kT'])
        attention(R, lambda g: kT[:, g, 0:128 + R], 128 + R,
                  [(lambda g: vprev[:, g * 64:(g + 1) * 64], 128, 'vprev'), (lambda g: kv_tok[:R, 256 + g * 64:256 + (g + 1) * 64], R, 'kv_tok')],
                  cm_ap, (4, 5))
        for half in range(2):
            P.op('dve', lambda e, half=half: e.tensor_tensor(out=og[:R, half * 512:(half + 1) * 512], in0=PB[4 + half][:R, :],
                                                            in1=sg[:R, half * 512:(half + 1) * 512], op=ALU.mult),
                 reads=[pb[4 + half], 'sg'], writes=['og'])
        if sample_seq is None:
            P.op('pool', lambda e: e.tensor_copy(out=kT[:, :, 0:128], in_=kT[:, :, 128:256]), reads=['kT'], writes=['kT'])
            P.op('pool', lambda e: e.tensor_copy(out=vprev[:], in_=kv_tok[:, 256:512]), reads=['kv_tok'], writes=['vprev'])
        if last_prompt:
            P.dma('pool', 'o_wk', o_wk[:, :], kv_tok[:, 0:256], reads=['kv_tok'])
            P.dma('pool', 'o_wv', o_wv[:, :], kv_tok[:, 256:512], reads=['kv_tok'])

    def l0_C(R, hn, H):
        transpose_to(og, 'og', ogT, 'ogT', (6, 7), nrow=R)
        project(ogT, 'ogT', a_wout, 0, (0, 1), nrow=R)
        for pi in range(2):
            P.op('dve', lambda e, pi=pi: e.tensor_tensor(out=H[:R, pi * 512:(pi + 1) * 512], in0=PB[pi][:R, :], in1=H[:R, pi * 512:(pi + 1) * 512],
                                                         op=ALU.add),
                 reads=[pb[pi], hn], writes=[hn])

    def TT(eng, out, in0, in1, op, reads, writes):
        P.op(eng, lambda e: e.tensor_tensor(out=out, in0=in0, in1=in1, op=op), reads=reads, writes=writes)

    def STT(out, in0, scalar, in1, op0, op1, reads, writes):
        P.op('dve', lambda e: e.scalar_tensor_tensor(out=out, in0=in0, scalar=scalar, in1=in1, op0=op0, op1=op1), reads=reads, writes=writes)

    def load_state(s_):
        P.dma('sp', 'Sld', Sld, d_swkv[s_].rearrange("h v k -> v h k"), writes=['Sld', 'AS0', 'AS1'])
        for hh in range(16):
            bk = 4 + hh // 8
            P.op('pe', lambda e, hh=hh, bk=bk: e.transpose(PB[bk][0:64, (hh % 8) * 64:(hh % 8) * 64 + 64], Sld[:, hh, :], ident[:64, :64]),
                 reads=['Sld', 'cst'], writes=[pb[bk]])
        for half in range(2):
            P.op('act', lambda e, half=half: e.copy(out=Hst[:, half * 8:(half + 1) * 8, :],
                                                   in_=PB[4 + half][0:64, :].rearrange("p (h v) -> p h v", h=8)),
                 reads=[pb[4 + half]], writes=['Hst'])

    def compute_gC(R, gsrc, gname):
        for hh in range(16):
            P.op('pe', lambda e, hh=hh: e.matmul(PB[7][0:64, hh:hh + 1], lhsT=gsrc[:R, hh * 64:(hh + 1) * 64], rhs=ident[:R, R - 1:R], start=True, stop=True),
                 reads=[gname, 'cst'], writes=[pb[7]])
        P.op('act', lambda e: e.copy(out=gC[:, :], in_=PB[7][0:64, 0:16]), reads=[pb[7]], writes=['gC'])

    def l1_tile(R, hn, H, sample_seq=None, last_prompt=False, yout=None, parts='ABC', batched=False):
        if 'A' in parts:
            l1_A(R, hn, H, last_prompt, batched)
        if 'B' in parts:
            l1_B(R, sample_seq, last_prompt)
        if 'C' in parts:
            l1_C(R, hn, H, yout)

    def l1_A(R, hn, H, last_prompt, batched):
        rms_rstd(H, hn, R)
        STT(xn[:R, :], H[:R, :], rstd[:R, :], vbc['g1'][:R, :], ALU.mult, ALU.mult, [hn, 'rstd', 'v_g1'], ['xn'])
        if not batched:
            P.op('pool', lambda e: e.tensor_copy(out=xnT1[:, :, 0:1], in_=xnT1[:, :, 128:129]), reads=['xnT1'], writes=['xnT1'])
            if last_prompt:
                P.dma('pool', 'o_shp', o_shp[:, :], xn[127:128, :], reads=['xn'])
            transpose_to(xn, 'xn', xnT1[:, :, 1:129], 'xnT1', (2, 3), nrow=R)
            TT('pool', dxT[:, :, :R], xnT1[:, :, 0:R], xnT1[:, :, 1:1 + R], ALU.subtract, ['xnT1'], ['xnT'])
        else:
            xp4 = xpT[:, :, :].rearrange("p c (s t) -> p c s t", t=4)
            P.dma('sp', 'k_tok', k_tok[0:16, :], d_ssh[:, :], writes=['k_tok'])
            for c in range(8):
                P.op('pe', lambda e, c=c: e.transpose(PB[0][:, c * 16:(c + 1) * 16], k_tok[0:16, c * 128:(c + 1) * 128], ident[:16, :16]),
                     reads=['k_tok', 'cst'], writes=[pb[0]])
            P.op('act', lambda e: e.copy(out=xp4[:, :, :, 0], in_=PB[0][:, 0:128].rearrange("p (c s) -> p c s", c=8)), reads=[pb[0]], writes=['xpT'])
            for s_ in range(NSEQ):
                P.dma('pool', 'o_shs', o_shs[s_:s_ + 1, :], xn[4 * s_ + 3:4 * s_ + 4, :], reads=['xn'])
            transpose_to(xn, 'xn', xnT1[:, :, 1:129], 'xnT1', (2, 3), nrow=R)
            x14 = xnT1[:, :, 1:65].rearrange("p c (s t) -> p c s t", t=4)
            P.op('pool', lambda e: e.tensor_copy(out=xp4[:, :, :, 1:4], in_=x14[:, :, :, 0:3]), reads=['xnT1'], writes=['xpT'])
            TT('pool', dxT[:, :, :R], xpT[:, :, :R], xnT1[:, :, 1:1 + R], ALU.subtract, ['xnT1', 'xpT'], ['xnT'])
        dsts = [r_tok, k_tok, v_tok, sgl]
        dnames = ['q_tok', 'k_tok', 'v_tok', 'sg']
        def emit_kk():
            TT('dve', kkb[:R, :], k_tok[:R, :], vbc['k_k'][:R, :], ALU.mult, ['k_tok', 'v_k_k'], ['S_sb'])
            TT('dve', tmpA[:R, :], kkb[:R, :], kkb[:R, :], ALU.mult, ['S_sb'], ['PTs'])
            P.op('dve', lambda e: e.tensor_reduce(out=s16[:R, 0:16], in_=v3(tmpA, R), axis=AX.X, op=ALU.add), reads=['PTs'], writes=['s16'])
            P.op('act', lambda e: e.activation(out=s16[:R, 0:16], in_=s16[:R, 0:16], func=AF.Sqrt), reads=['s16'], writes=['s16'])
            P.op('dve', lambda e: e.tensor_scalar(out=s16[:R, 0:16], in0=s16[:R, 0:16], scalar1=1e-12, scalar2=None, op0=ALU.max), reads=['s16'], writes=['s16'])
            P.op('dve', lambda e: e.reciprocal(out=s16[:R, 0:16], in_=s16[:R, 0:16]), reads=['s16'], writes=['s16'])
            TT('dve', v3(kkb, R), v3(kkb, R), bc16(s16, R), ALU.mult, ['S_sb', 's16'], ['S_sb'])

        def emit_kh():
            STT(tmpA[:R, :], asg[:R, :], -1.0, vbc['k_a'][:R, :], ALU.add, ALU.mult, ['ogT', 'v_k_a'], ['PTs'])
            STT(k_tok[:R, :], tmpA[:R, :], 1.0, k_tok[:R, :], ALU.add, ALU.mult, ['PTs', 'k_tok'], ['k_tok'])
            TT('dve', tmpA[:R, :], r_tok[:R, :], k_tok[:R, :], ALU.mult, ['q_tok', 'k_tok'], ['PTs'])
            TT('dve', tmpA[:R, :], tmpA[:R, :], vbc['r_k'][:R, :], ALU.mult, ['PTs', 'v_r_k'], ['PTs'])
            P.op('dve', lambda e: e.tensor_reduce(out=s16[:R, 16:32], in_=v3(tmpA, R), axis=AX.X, op=ALU.add), reads=['PTs'], writes=['s16b'])
            TT('dve', v3(tmpA, R), v3(v_tok, R), s16[:R, 16:32].unsqueeze(2).to_broadcast([R, 16, 64]), ALU.mult, ['v_tok', 's16b'], ['PTs'])

        for c in range(6):
            X = xm[c % 2]
            xname = f'xm{c % 2}'
            TT('pool', X[:, :, :R], dxT[:, :, :R], muT[:, c * 8:(c + 1) * 8].unsqueeze(2).to_broadcast([128, 8, R]), ALU.mult,
               ['xnT', 'muT'], [xname])
            TT('pool', X[:, :, :R], X[:, :, :R], xnT1[:, :, 1:1 + R], ALU.add, [xname, 'xnT1'], [xname])
            if c < 4:
                bks = (0, 1) if c % 2 == 0 else (4, 5)
                project(X, xname, r_win[c], 0, bks, nrow=R)
                for pi in range(2):
                    if c == 3:
                        P.op('act', lambda e, pi=pi, bks=bks: e.activation(out=sgl[:R, pi * 512:(pi + 1) * 512], in_=PB[bks[pi]][:R, :], func=AF.Silu),
                             reads=[pb[bks[pi]]], writes=['sg'])
                    else:
                        P.op('act', lambda e, pi=pi, c=c, bks=bks: e.copy(out=dsts[c][:R, pi * 512:(pi + 1) * 512], in_=PB[bks[pi]][:R, :]),
                             reads=[pb[bks[pi]]], writes=[dnames[c]])
                if c == 1:
                    emit_kk()
            else:
                ws, wn, tt_, tn, w2, w2n, dst, dn = (w1s, 'w1s', tw, 'tw', w2s, 'w2s', sgz, 'og') if c == 4 else (a1s, 'a1s', ta, 'ta', a2s, 'a2s', asg, 'ogT')
                for kc in range(8):
                    P.op('pe', lambda e, kc=kc, ws=ws, X=X: e.matmul(PB[2][0:64, :R], lhsT=ws[:, kc, :], rhs=X[:, kc, :R], start=(kc == 0), stop=(kc == 7)),
                         reads=[wn, xname], writes=[pb[2]])
                P.op('act', lambda e, tt_=tt_, c=c: e.activation(out=tt_[0:64, :R], in_=PB[2][0:64, :R], func=(AF.Tanh if c == 4 else AF.Copy)),
                     reads=[pb[2]], writes=[tn])
                for pi in range(2):
                    P.op('pe', lambda e, pi=pi, tt_=tt_, w2=w2: e.matmul(PB[pi][:R, :], lhsT=tt_[:, :R], rhs=w2[:, pi * 512:(pi + 1) * 512], start=True, stop=True),
                         reads=[tn, w2n], writes=[pb[pi]])
                    P.op('act', lambda e, pi=pi, dst=dst: e.activation(out=dst[:R, pi * 512:(pi + 1) * 512], in_=PB[pi][:R, :], func=AF.Sigmoid),
                         reads=[pb[pi]], writes=[dn])
                if c == 5:
                    emit_kh()
        for pi in range(2):
            P.op('pe', lambda e, pi=pi: e.matmul(PB[pi][:R, :], lhsT=(BDm if batched else UI)[:R, :R], rhs=sgz[:R, pi * 512:(pi + 1) * 512], start=True, stop=True),
                 reads=['cst', 'og'], writes=[pb[pi]])
            sl = slice(pi * 512, (pi + 1) * 512)
            P.op('act', lambda e, pi=pi, sl=sl: e.activation(out=gam[:R, sl], in_=PB[pi][:R, :], func=AF.Exp, scale=-CDEC), reads=[pb[pi]], writes=['xm0'])
            P.op('act', lambda e, pi=pi, sl=sl: e.activation(out=ginv[:R, sl], in_=PB[pi][:R, :], func=AF.Exp, scale=CDEC), reads=[pb[pi]], writes=['xm1'])
        P.op('act', lambda e: e.activation(out=sgz[:R, :], in_=sgz[:R, :], func=AF.Exp, scale=CDEC), reads=['og'], writes=['og'])
        TT('dve', sgz[:R, :], sgz[:R, :], gam[:R, :], ALU.mult, ['og', 'xm0'], ['og'])
        STT(sgz[:R, :], kkb[:R, :], -1.0, sgz[:R, :], ALU.mult, ALU.mult, ['S_sb', 'og'], ['og'])
        TT('dve', kkb[:R, :], kkb[:R, :], asg[:R, :], ALU.mult, ['S_sb', 'ogT'], ['S_sb'])
        TT('dve', kkb[:R, :], kkb[:R, :], ginv[:R, :], ALU.mult, ['S_sb', 'xm1'], ['S_sb'])
        TT('dve', ginv[:R, :], k_tok[:R, :], ginv[:R, :], ALU.mult, ['k_tok', 'xm1'], ['xm1'])
        if not batched:
            compute_gC(R, gam, 'xm0')
        else:
            P.dma('pool', 'sc_gam', sc['gam'][:, :], gam[:R, :], reads=['xm0'], writes=['sc_gam'])
        TT('dve', gam[:R, :], r_tok[:R, :], gam[:R, :], ALU.mult, ['q_tok', 'xm0'], ['xm0'])

    def l1_B(R, sample_seq, last_prompt):
        nlev = int(round(math.log2(R))) - 1
        if sample_seq is not None:
            load_state(sample_seq)
            compute_gC(R, xn, 'xn')
        At_, Rt_, Bt_, Kt_ = sgz, gam, kkb, ginv
        for hf in range(2):
            for (src, sname, dstf, dname, banks) in [(At_, 'og', lambda h8: ART[:, h8, 0, :R], 'ART', (0, 1)), (Bt_, 'S_sb', lambda h8: BT[:, h8, :R], 'qT', (4, 5)),
                                                     (Kt_, 'xm1', lambda h8: KT[:, h8, :R], 'KT', (6, 7)), (Rt_, 'xm0', lambda h8: ART[:, h8, 1, :R], 'ART', (2, 3))]:
                for h8 in range(8):
                    hh = hf * 8 + h8
                    bk = banks[h8 // 4]
                    P.op('pe', lambda e, src=src, hh=hh, h8=h8, bk=bk: e.transpose(PB[bk][0:64, (h8 % 4) * 128:(h8 % 4) * 128 + R],
                                                                             src[:R, hh * 64:(hh + 1) * 64], ident[:R, :R]),
                         reads=[sname, 'cst'], writes=[pb[bk]])
                for q2 in range(2):
                    bk = banks[q2]
                    if dname == 'ART':
                        which = 0 if src is At_ else 1
                        outap = ART[:, q2 * 4:(q2 + 1) * 4, which, :R]
                    elif dname == 'qT':
                        outap = BT[:, q2 * 4:(q2 + 1) * 4, :R]
                    else:
                        outap = KT[:, q2 * 4:(q2 + 1) * 4, :R]
                    P.op('act' if q2 == 0 else 'dve',
                         (lambda e, outap=outap, bk=bk: e.copy(out=outap, in_=PB[bk][0:64, :].rearrange("p (c t) -> p c t", c=4)[:, :, :R])) if q2 == 0 else
                         (lambda e, outap=outap, bk=bk: e.tensor_copy(out=outap, in_=PB[bk][0:64, :].rearrange("p (c t) -> p c t", c=4)[:, :, :R])),
                         reads=[pb[bk]], writes=[dname])
            for hs in range(2):
                heads = [hf * 8 + hs * 4 + i for i in range(4)]
                for i, hh in enumerate(heads):
                    h8 = hh % 8
                    o3 = lambda c0, i=i: PB[i][:R, c0:c0 + 256].rearrange("p (a t) -> p a t", a=2)[:, :, :R]
                    P.op('pe', lambda e, h8=h8, o3=o3: e.matmul(o3(0), lhsT=BT[:, h8, :R], rhs=ART[:, h8, :, :R], start=True, stop=True),
                         reads=['qT', 'ART'], writes=[pb[i]])
                    P.op('pe', lambda e, h8=h8, o3=o3: e.matmul(o3(256), lhsT=KT[:, h8, :R], rhs=ART[:, h8, :, :R], start=True, stop=True),
                         reads=['KT', 'ART'], writes=[pb[i]])
                    P.op('pe', lambda e, h8=h8, i=i: e.matmul(PB[4][:R, i * 128:i * 128 + R], lhsT=ART[:, h8, 0, :R], rhs=BT[:, h8, :R], start=True, stop=True),
                         reads=['qT', 'ART'], writes=[pb[4]])
                QX, PP = QP, XS
                qn = lambda b_, hf_: ('QPa', 'QPb')[b_] + str(hf_)
                pn = lambda b_, hf_: ('XSa', 'XSb')[b_] + str(hf_)
                pbank = (2, 3)
                for i in range(4):
                    hf_ = i // 2
                    TT('dve', AS[i][:R, :], PB[i][:R, :], M4[:R, :], ALU.mult, [pb[i], 'cst'], [f'AS{i}'] + (['Sld'] if i < 2 else []))
                    P.op('pool', lambda e, i=i: e.tensor_copy(out=QX[0][:R, i * 256:i * 256 + R], in_=AS[i][:R, 0:R]), reads=[f'AS{i}'], writes=[qn(0, hf_)])
                    P.op('pool', lambda e, i=i: e.tensor_copy(out=QX[0][:R, i * 256 + 128:i * 256 + 128 + R], in_=ident[:R, :R]), reads=['cst'], writes=[qn(0, hf_)])
                    TT('dve', PP[0][:R, i * 128:i * 128 + R], PB[4][:R, i * 128:i * 128 + R], LS[:R, :R], ALU.mult, [pb[4], 'cst'], [pn(0, hf_)])
                v2 = lambda ap, a: ap.rearrange("p (h a t) -> p h a t", a=2, t=128)[:, :, a, :R]
                for lv in range(1, nlev + 2):
                    sb_, db_ = (lv - 1) % 2, lv % 2
                    last = (lv == nlev + 1)
                    for hf_ in range(2):
                        for i in (2 * hf_, 2 * hf_ + 1):
                            c0 = (i % 2) * 256
                            Ppv = PP[sb_][:R, i * 128:i * 128 + R]
                            if not last:
                                P.op('pe', lambda e, i=i, hf_=hf_, c0=c0, Ppv=Ppv, sb_=sb_: e.matmul(
                                    PB[hf_][:R, c0:c0 + 256].rearrange("p (a t) -> p a t", a=2)[:, :, :R], lhsT=Ppv,
                                    rhs=QX[sb_][:R, i * 256:i * 256 + 256].rearrange("p (a t) -> p a t", a=2)[:, :, :R], start=True, stop=True),
                                    reads=[qn(sb_, hf_), pn(sb_, hf_)], writes=[pb[hf_]])
                                P.op('pe', lambda e, i=i, hf_=hf_, c0=c0, Ppv=Ppv, sb_=sb_: e.matmul(
                                    PB[pbank[hf_]][:R, (i % 2) * 128:(i % 2) * 128 + R], lhsT=QX[sb_][:R, i * 256:i * 256 + R], rhs=Ppv, start=True, stop=True),
                                    reads=[qn(sb_, hf_), pn(sb_, hf_)], writes=[pb[pbank[hf_]]])
                            else:
                                P.op('pe', lambda e, i=i, hf_=hf_, Ppv=Ppv, sb_=sb_: e.matmul(
                                    PB[pbank[hf_]][:R, (i % 2) * 128:(i % 2) * 128 + R], lhsT=Ppv, rhs=QX[sb_][:R, i * 256 + 128:i * 256 + 128 + R],
                                    start=True, stop=True),
                                    reads=[qn(sb_, hf_), pn(sb_, hf_)], writes=[pb[pbank[hf_]]])
                    for hf_ in range(2):
                        ssl = QX[sb_][:R, hf_ * 512:(hf_ + 1) * 512]
                        if not last:
                            dsl = QX[db_][:R, hf_ * 512:(hf_ + 1) * 512]
                            P.op('dve', lambda e, hf_=hf_, dsl=dsl: e.tensor_copy(out=v2(dsl, 0), in_=v2(PB[hf_][:R, :], 0)), reads=[pb[hf_]], writes=[qn(db_, hf_)])
                            TT('dve', v2(dsl, 1), v2(ssl, 1), v2(PB[hf_][:R, :], 1), ALU.add, [qn(sb_, hf_), pb[hf_]], [qn(db_, hf_)])
                            P.op('act', lambda e, hf_=hf_, db_=db_: e.copy(out=PP[db_][:R, hf_ * 256:(hf_ + 1) * 256], in_=PB[pbank[hf_]][:R, 0:256]),
                                 reads=[pb[pbank[hf_]]], writes=[pn(db_, hf_)])
                        else:
                            TT('dve', v2(ssl, 1), v2(ssl, 1), PB[pbank[hf_]][:R, 0:256].rearrange("p (h t) -> p h t", t=128)[:, :, :R], ALU.add,
                               [qn(sb_, hf_), pb[pbank[hf_]]], [qn(sb_, hf_)])
                cb_ = nlev % 2
                XFt = QX[cb_]
                xfn = lambda i: qn(cb_, i // 2)
                for i, hh in enumerate(heads):
                    h8 = hh % 8
                    P.op('pe', lambda e, i=i, hh=hh, h8=h8: e.matmul(PB[5][:R, i * 64:(i + 1) * 64], lhsT=ART[:, h8, 0, :R], rhs=Hst[:, hh, :], start=True, stop=False),
                         reads=['ART', 'Hst'], writes=[pb[5]])
                    P.op('pe', lambda e, i=i, hh=hh: e.matmul(PB[5][:R, i * 64:(i + 1) * 64], lhsT=AS[i][:R, 256:256 + R], rhs=v_tok[:R, hh * 64:(hh + 1) * 64],
                                                            start=False, stop=True),
                         reads=[f'AS{i}', 'v_tok'], writes=[pb[5]])
                P.op('act', lambda e: e.copy(out=Wsb[:R, :], in_=PB[5][:R, 0:256]), reads=[pb[5]], writes=['Wsb'])
                for i, hh in enumerate(heads):
                    P.op('pe', lambda e, i=i: e.matmul(PB[5][:R, 256 + i * 64:256 + (i + 1) * 64], lhsT=XFt[:R, i * 256 + 128:i * 256 + 128 + R], rhs=Wsb[:R, i * 64:(i + 1) * 64],
                                                     start=True, stop=True),
                         reads=[xfn(i), 'Wsb'], writes=[pb[5]])
                P.op('act', lambda e: e.copy(out=Usb[:R, :], in_=PB[5][:R, 256:512]), reads=[pb[5]], writes=['Usb'])
                for i, hh in enumerate(heads):
                    h8 = hh % 8
                    vv = v_tok[:R, hh * 64:(hh + 1) * 64]
                    uu = Usb[:R, i * 64:(i + 1) * 64]
                    yo = PB[6][:R, i * 64:(i + 1) * 64]
                    ho = PB[6][0:64, 256 + i * 64:256 + (i + 1) * 64]
                    P.op('pe', lambda e, yo=yo, h8=h8, hh=hh: e.matmul(yo, lhsT=ART[:, h8, 1, :R], rhs=Hst[:, hh, :], start=True, stop=False),
                         reads=['ART', 'Hst'], writes=[pb[6]])
                    P.op('pe', lambda e, yo=yo, i=i, uu=uu: e.matmul(yo, lhsT=AS[i][:R, 128:128 + R], rhs=uu, start=False, stop=False),
                         reads=[f'AS{i}', 'Usb'], writes=[pb[6]])
                    P.op('pe', lambda e, yo=yo, i=i, vv=vv: e.matmul(yo, lhsT=AS[i][:R, 384:384 + R], rhs=vv, start=False, stop=True),
                         reads=[f'AS{i}', 'v_tok'], writes=[pb[6]])
                    P.op('pe', lambda e, ho=ho, hh=hh, uu=uu: e.matmul(ho, lhsT=Bt_[:R, hh * 64:(hh + 1) * 64], rhs=uu, start=True, stop=False),
                         reads=['S_sb', 'Usb'], writes=[pb[6]])
                    P.op('pe', lambda e, ho=ho, hh=hh, vv=vv: e.matmul(ho, lhsT=Kt_[:R, hh * 64:(hh + 1) * 64], rhs=vv, start=False, stop=True),
                         reads=['xm1', 'v_tok'], writes=[pb[6]])
                h0 = heads[0]
                P.op('act', lambda e, h0=h0: e.copy(out=ysb[:R, h0 * 64:(h0 + 4) * 64], in_=PB[6][:R, 0:256]), reads=[pb[6]], writes=['junk', 'tok6'])
                TT('dve', Hst[:, h0:h0 + 4, :], PB[6][0:64, 256:512].rearrange("p (h v) -> p h v", h=4), Hst[:, h0:h0 + 4, :], ALU.add, [pb[6], 'Hst', 'tok6'], ['Hst'])
                TT('dve', Hst[:, h0:h0 + 4, :], Hst[:, h0:h0 + 4, :], gC[:, h0:h0 + 4].unsqueeze(2).to_broadcast([64, 4, 64]), ALU.mult, ['Hst', 'gC'], ['Hst'])
        if last_prompt or sample_seq is not None:
            store_state(sample_seq)

    def l1_C(R, hn, H, yout):
        P.op('dve', lambda e: e.tensor_reduce(out=s16[:R, 32:48], in_=v3(ysb, R), axis=AX.X, op=ALU.add), reads=['junk'], writes=['s16c'])
        P.op('dve', lambda e: e.tensor_scalar(out=s16[:R, 32:48], in0=s16[:R, 32:48], scalar1=-1.0 / 64, scalar2=None, op0=ALU.mult), reads=['s16c'], writes=['s16c'])
        TT('dve', v3(ysb, R), v3(ysb, R), s16[:R, 32:48].unsqueeze(2).to_broadcast([R, 16, 64]), ALU.add, ['junk', 's16c'], ['junk'])
        TT('dve', k_tok[:R, :], ysb[:R, :], ysb[:R, :], ALU.mult, ['junk'], ['k_tok'])
        P.op('dve', lambda e: e.tensor_reduce(out=s16[:R, 48:64], in_=v3(k_tok, R), axis=AX.X, op=ALU.add), reads=['k_tok'], writes=['s16d'])
        P.op('dve', lambda e: e.tensor_scalar(out=s16[:R, 48:64], in0=s16[:R, 48:64], scalar1=1.0 / 64, scalar2=64e-5, op0=ALU.mult, op1=ALU.add),
             reads=['s16d'], writes=['s16d'])
        P.op('act', lambda e: e.activation(out=s16[:R, 48:64], in_=s16[:R, 48:64], func=AF.Sqrt), reads=['s16d'], writes=['s16d'])
        P.op('dve', lambda e: e.reciprocal(out=s16[:R, 48:64], in_=s16[:R, 48:64]), reads=['s16d'], writes=['s16d'])
        TT('dve', v3(ysb, R), v3(ysb, R), s16[:R, 48:64].unsqueeze(2).to_broadcast([R, 16, 64]), ALU.mult, ['junk', 's16d'], ['junk'])
        TT('dve', ysb[:R, :], ysb[:R, :], vbc['ln_g'][:R, :], ALU.mult, ['junk', 'v_ln_g'], ['junk'])
        TT('dve', ysb[:R, :], ysb[:R, :], vbc['ln_b'][:R, :], ALU.add, ['junk', 'v_ln_b'], ['junk'])
        TT('dve', ysb[:R, :], ysb[:R, :], tmpA[:R, :], ALU.add, ['junk', 'PTs'], ['junk'])
        TT('dve', ysb[:R, :], ysb[:R, :], sgl[:R, :], ALU.mult, ['junk', 'sg'], ['junk'])
        transpose_to(ysb, 'junk', xnT, 'xnT', (2, 3), nrow=R)
        project(xnT, 'xnT', r_wout, 0, (0, 1), nrow=R)
        for pi in range(2):
            TT('dve', H[:R, pi * 512:(pi + 1) * 512], PB[pi][:R, :], H[:R, pi * 512:(pi + 1) * 512], ALU.add, [pb[pi], hn], [hn])
        if yout is not None:
            rms_rstd(H, hn, R)
            STT(xn[:R, :], H[:R, :], rstd[:R, :], vbc['gf'][:R, :], ALU.mult, ALU.mult, [hn, 'rstd', 'v_gf'], ['xn'])
            P.dma('pool', 'o_y', yout, xn[:R, :], reads=['xn'])

    def store_state(sample_seq):
        for hh in range(16):
            bk = 4 + hh // 8
            P.op('pe', lambda e, hh=hh, bk=bk: e.transpose(PB[bk][0:64, (hh % 8) * 64:(hh % 8) * 64 + 64], Hst[:, hh, :], ident[:64, :64]),
                 reads=['Hst', 'cst'], writes=[pb[bk]])
        for half in range(2):
            P.op('act', lambda e, half=half: e.copy(out=Sld[:, half * 8:(half + 1) * 8, :], in_=PB[4 + half][0:64, :].rearrange("p (h k) -> p h k", h=8)),
                 reads=[pb[4 + half]], writes=['Sld', 'AS0', 'AS1'])
        dst = o_wkv if sample_seq is None else o_wkvs[sample_seq]
        P.dma('pool', 'o_S', dst.rearrange("h v k -> v h k"), Sld, reads=['Sld', 'AS0', 'AS1'])

    nt = NT if stage != 1 else 3
    for t in range(nt):
        hn = 'hA' if t % 2 == 0 else 'hB'
        H = hb[t % 2]
        cm_ap = cst[:, C_CM0:C_CM0 + 256] if t == 0 else (cst[:, C_CM1:C_CM1 + 256] if t == 1 else None)
        l0_tile(128, xp[t * 128:(t + 1) * 128, :], hn, H, cm_ap, last_prompt=(t == nt - 1))
        if stage == 0:
            if t >= 1:
                P.dma('pool', 'o_y' + hn, yp[(t - 1) * 128:t * 128, :], H[:], reads=[hn])
        else:
            l1_tile(128, hn, H, last_prompt=(t == nt - 1), yout=(yp[(t - 1) * 128:t * 128, :] if t >= 1 else None))
    if stage >= 2:
        H, hn = hb[0], 'hA'
        l0_tile(NTOK, d_xs[:, :], hn, H, None, parts='A')
        for nm, src, sn in [('q', q_tok, 'q_tok'), ('kv', kv_tok, 'kv_tok'), ('sg', sg, 'sg')]:
            P.dma('pool', 'sc_' + nm, sc[nm][:, :], src[:NTOK, :], reads=[sn], writes=['sc_' + nm])
        for s_ in range(NSEQ):
            r0 = 4 * s_
            for nm, dst, dn in [('q', q_tok, 'q_tok'), ('kv', kv_tok, 'kv_tok'), ('sg', sg, 'sg')]:
                P.dma('sp', 'ld_' + nm, dst[0:4, :], sc[nm][r0:r0 + 4, :], reads=['sc_' + nm], writes=[dn])
            l0_tile(4, None, hn, H, None, sample_seq=s_, parts='B')
            P.dma('pool', 'sc_og', sc['og'][r0:r0 + 4, :], og[0:4, :], reads=['og'], writes=['sc_og'])
        P.dma('sp', 'ld_og', og[:NTOK, :], sc['og'][:, :], reads=['sc_og'], writes=['og'])
        l0_tile(NTOK, None, hn, H, None, parts='C')
        l1_tile(NTOK, hn, H, parts='A', batched=True)
        l1src = [('At', sgz, 'og'), ('Rt', gam, 'xm0'), ('Bt', kkb, 'S_sb'), ('Kt', ginv, 'xm1'), ('v', v_tok, 'v_tok')]
        for nm, src, sn in l1src:
            P.dma('pool', 'sc_' + nm, sc[nm][:, :], src[:NTOK, :], reads=[sn], writes=['sc_' + nm])
        for s_ in range(NSEQ):
            r0 = 4 * s_
            for nm, dst, dn in l1src + [('gam', xn, 'xn')]:
                P.dma('sp', 'ld_' + nm, dst[0:4, :], sc[nm][r0:r0 + 4, :], reads=['sc_' + nm], writes=[dn])
            l1_tile(4, hn, H, sample_seq=s_, parts='B')
            P.dma('pool', 'sc_y', sc['y'][r0:r0 + 4, :], ysb[0:4, :], reads=['junk'], writes=['sc_y'])
        P.dma('sp', 'ld_y', ysb[:NTOK, :], sc['y'][:, :], reads=['sc_y'], writes=['junk'])
        l1_tile(NTOK, hn, H, parts='C', yout=ys[:, :])
    P.finish('pool')
    info = dict(cnt=dict(P.cnt), nwaits=P.nwaits, sbuf_left=nc.sbuf_bytes_remaining)
    return nc, st, info


def kernel(x_prompt, x_sample, cache_win_k, cache_win_v, state_wkv, state_shift,
           meta_tokens, rel_bias_table, norm_gain, final_gain,
           attn_w_in, attn_sinks, attn_w_out,
           rwkv_mu, rwkv_w_in, rwkv_w0, rwkv_w1, rwkv_w2, rwkv_a0, rwkv_a1, rwkv_a2,
           rwkv_k_k, rwkv_k_a, rwkv_r_k, rwkv_ln_gamma, rwkv_ln_beta, rwkv_w_out, _stage=9):
    f = lambda a: np.ascontiguousarray(np.asarray(a, dtype=np.float32))
    nc, st, info = build(_stage)
    cst, E = host_consts()
    vecs = np.stack([f(norm_gain)[0], f(norm_gain)[1], f(final_gain), f(rwkv_k_k)[0], f(rwkv_k_a)[0],
                     f(rwkv_r_k)[0].reshape(-1), f(rwkv_ln_gamma)[0], f(rwkv_ln_beta)[0]])
    tab33 = np.concatenate([f(rel_bias_table), np.full((1, 16), NEG, np.float32)], 0)
    shared = dict(a_win=f(attn_w_in)[0], a_wout=f(attn_w_out)[0], r_win=f(rwkv_w_in)[0], r_wout=f(rwkv_w_out)[0],
                  w1=f(rwkv_w1)[0], a1=f(rwkv_a1)[0],
                  w2w0=np.concatenate([f(rwkv_w2)[0], f(rwkv_w0)], 0), a2a0=np.concatenate([f(rwkv_a2)[0], f(rwkv_a0)], 0),
                  vecs=f(vecs), mu=f(rwkv_mu)[0], tab33=f(tab33), sinks=f(attn_sinks), cst=cst, E=E)
    pad = np.zeros((112, 1024), np.float32)
    in_maps = []
    for c in range(8):
        m = dict(shared)
        m["xp"] = np.ascontiguousarray(np.concatenate([pad, f(meta_tokens), f(x_prompt)[c % 4]], 0)[:NT * 128])
        sq = slice(NSEQ * c, NSEQ * (c + 1))
        m["xs"] = np.ascontiguousarray(f(x_sample)[sq].reshape(NTOK, 1024))
        m["ck"] = np.ascontiguousarray(f(cache_win_k)[0, sq].reshape(NSEQ, 128, 256))
        m["cv"] = np.ascontiguousarray(f(cache_win_v)[0, sq].reshape(NSEQ, 128, 256))
        m["swkv"] = np.ascontiguousarray(f(state_wkv)[0, sq])
        m["ssh"] = np.ascontiguousarray(f(state_shift)[0, sq])
        in_maps.append(m)
    res = run_bass_kernel_spmd(nc, in_maps, core_ids=list(range(8)))
    st.close()
    R = res.results
    y_prompt = np.stack([R[b]["yp"] for b in range(4)])
    wk = np.stack([R[b]["o_wk"].reshape(128, 4, 64) for b in range(4)])[None]
    wv = np.stack([R[b]["o_wv"].reshape(128, 4, 64) for b in range(4)])[None]
    wkv_p = np.stack([R[b]["o_wkv"] for b in range(4)])[None]
    sh_p = np.stack([R[b]["o_shp"][0] for b in range(4)])[None]
    y_sample = np.concatenate([R[c]["ys"].reshape(NSEQ, 4, 1024) for c in range(8)], 0)
    wks = np.concatenate([R[c]["o_wks"].reshape(NSEQ, 128, 4, 64) for c in range(8)], 0)[None]
    wvs = np.concatenate([R[c]["o_wvs"].reshape(NSEQ, 128, 4, 64) for c in range(8)], 0)[None]
    wkvs = np.concatenate([R[c]["o_wkvs"] for c in range(8)], 0)[None]
    shs = np.concatenate([R[c]["o_shs"] for c in range(8)], 0)[None]
    return (y_prompt, y_sample, wk, wv, wkv_p, sh_p, wks, wvs, wkvs, shs)
```
